# Optimizing a Trainium2 kernel written in Bass

```python
import math
import jax, jax.numpy as jnp
from jax import lax
import numpy as np

D_MODEL = 1024
BATCH = 8
SEQ = 4096
DEPTH = 2

CHUNK = 64
N_LEFT_CHUNKS = 8
BAND = (N_LEFT_CHUNKS + 1) * CHUNK
Q_BLOCK = 128
HEAD_DIM = 64
WIDTH_A = D_MODEL // 2
WIDTH_B = D_MODEL - WIDTH_A
N_HEADS_A = WIDTH_A // HEAD_DIM
N_HEADS_B = WIDTH_B // (2 * HEAD_DIM)
REL_CLIP = 128
ROPE_THETA = 10000.0
DEEPNORM_ALPHA = (2 * DEPTH) ** 0.25
DEEPNORM_BETA = (8 * DEPTH) ** -0.25
LN_EPS = 1e-5
RMS_EPS = 1e-5
NEG_INF = -1e30
PROJ_COLS = 4 * WIDTH_A + 4 * WIDTH_B

kernel_name = "hybrid_chunked_relpos_diff_attention_deepnorm"


def layer_norm(x, g, b):
    xf = x.astype(jnp.float32)
    mu = jnp.mean(xf, axis=-1, keepdims=True)
    var = jnp.mean(jnp.square(xf - mu), axis=-1, keepdims=True)
    y = (xf - mu) * lax.rsqrt(var + LN_EPS) * g.astype(jnp.float32) + b.astype(jnp.float32)
    return y.astype(x.dtype)


def rms_norm(x, g):
    xf = x.astype(jnp.float32)
    y = xf * lax.rsqrt(jnp.mean(jnp.square(xf), axis=-1, keepdims=True) + RMS_EPS) * g.astype(jnp.float32)
    return y.astype(x.dtype)


def rope(x, pos):
    d = x.shape[-1]
    inv_freq = ROPE_THETA ** (-jnp.arange(0, d, 2, dtype=jnp.float32) / d)
    ang = pos[:, None] * inv_freq[None, :]
    ang = jnp.concatenate([ang, ang], axis=-1)[None, :, None, None, :]
    xf = x.astype(jnp.float32)
    x1, x2 = jnp.split(xf, 2, axis=-1)
    rot = jnp.concatenate([-x2, x1], axis=-1)
    return (xf * jnp.cos(ang) + rot * jnp.sin(ang)).astype(x.dtype)


def chunked_relpos_attention(q, k, v, bias_table):
    B, S, H, D = q.shape
    n_chunks = S // CHUNK
    pad = BAND - CHUNK
    k_pad = jnp.pad(k, ((0, 0), (pad, 0), (0, 0), (0, 0)))
    v_pad = jnp.pad(v, ((0, 0), (pad, 0), (0, 0), (0, 0)))
    rel = pad + jnp.arange(CHUNK)[:, None] - jnp.arange(BAND)[None, :]
    bias = bias_table[:, jnp.clip(rel, -REL_CLIP, REL_CLIP) + REL_CLIP].astype(jnp.float32)
    q_chunks = q.reshape(B, n_chunks, CHUNK, H, D).transpose(1, 0, 2, 3, 4)
    scale = D ** -0.5

    def one_chunk(args):
        qc, c = args
        start = c * CHUNK
        kc = lax.dynamic_slice_in_dim(k_pad, start, BAND, axis=1)
        vc = lax.dynamic_slice_in_dim(v_pad, start, BAND, axis=1)
        s = jnp.einsum('bqhd,bkhd->bhqk', qc, kc).astype(jnp.float32) * scale + bias[None]
        valid = (start - pad + jnp.arange(BAND)) >= 0
        s = jnp.where(valid[None, None, None, :], s, NEG_INF)
        p = jax.nn.softmax(s, axis=-1).astype(vc.dtype)
        return jnp.einsum('bhqk,bkhd->bqhd', p, vc)

    out = lax.map(one_chunk, (q_chunks, jnp.arange(n_chunks)))
    return out.transpose(1, 0, 2, 3, 4).reshape(B, S, H * D)


def differential_attention(q, k, v, lam, subln_g, lam_init):
    B, S, H, _, D = q.shape
    n_blocks = S // Q_BLOCK
    q_blocks = q.reshape(B, n_blocks, Q_BLOCK, H, 2, D).transpose(1, 0, 2, 3, 4, 5)
    key_chunk = jnp.arange(S) // CHUNK
    scale = D ** -0.5

    def one_block(args):
        qb, i = args
        q_chunk = (i * Q_BLOCK + jnp.arange(Q_BLOCK)) // CHUNK
        s = jnp.einsum('bqhnd,bkhnd->bhnqk', qb, k).astype(jnp.float32) * scale
        mask = key_chunk[None, :] <= q_chunk[:, None]
        s = jnp.where(mask[None, None, None], s, NEG_INF)
        p = jax.nn.softmax(s, axis=-1)
        a = (p[:, :, 0] - lam * p[:, :, 1]).astype(v.dtype)
        return jnp.einsum('bhqk,bkhe->bqhe', a, v)

    o = lax.map(one_block, (q_blocks, jnp.arange(n_blocks)))
    o = o.transpose(1, 0, 2, 3, 4).reshape(B, S, H, 2 * D)
    o = rms_norm(o, subln_g) * (1.0 - lam_init)
    return o.reshape(B, S, H * 2 * D)


def hybrid_layer(x, w_in, w_out, rel_bias, lq1, lk1, lq2, lk2, subln_g, ln_g, ln_b, layer_idx):
    B, S, _ = x.shape
    h = jnp.einsum('bsd,dc->bsc', x, w_in)
    h_a, h_b = h[..., :4 * WIDTH_A], h[..., 4 * WIDTH_A:]
    qa, ka, va, ga = jnp.split(h_a, 4, axis=-1)
    qb, kb, vb, gb = jnp.split(h_b, 4, axis=-1)

    shp_a = (B, S, N_HEADS_A, HEAD_DIM)
    y_a = chunked_relpos_attention(qa.reshape(shp_a), ka.reshape(shp_a), va.reshape(shp_a), rel_bias)
    y_a = y_a * jax.nn.silu(ga)

    pos = jnp.arange(S, dtype=jnp.float32)
    shp_b = (B, S, N_HEADS_B, 2, HEAD_DIM)
    qb = rope(qb.reshape(shp_b), pos)
    kb = rope(kb.reshape(shp_b), pos)
    vb = vb.reshape(B, S, N_HEADS_B, 2 * HEAD_DIM)
    lam_init = 0.8 - 0.6 * math.exp(-0.3 * layer_idx)
    lam = (jnp.exp(jnp.sum(lq1.astype(jnp.float32) * lk1.astype(jnp.float32)))
           - jnp.exp(jnp.sum(lq2.astype(jnp.float32) * lk2.astype(jnp.float32))) + lam_init)
    y_b = differential_attention(qb, kb, vb, lam, subln_g, lam_init)
    y_b = y_b * jax.nn.silu(gb)

    y = jnp.einsum('bsc,cd->bsd', jnp.concatenate([y_a, y_b], axis=-1), w_out)
    return layer_norm(DEEPNORM_ALPHA * x + y, ln_g, ln_b)


def setup_inputs(seed: int = 0) -> dict:
    key = jax.random.key(seed)
    ks = jax.random.split(key, 12)
    x = jax.random.normal(ks[0], (BATCH, SEQ, D_MODEL), jnp.float32)
    col_scale = np.ones((PROJ_COLS,), np.float32)
    col_scale[2 * WIDTH_A:3 * WIDTH_A] = DEEPNORM_BETA
    col_scale[4 * WIDTH_A + 2 * WIDTH_B:4 * WIDTH_A + 3 * WIDTH_B] = DEEPNORM_BETA
    w_in = jax.random.normal(ks[1], (DEPTH, D_MODEL, PROJ_COLS), jnp.float32) * (D_MODEL ** -0.5) * jnp.asarray(col_scale)
    w_out = jax.random.normal(ks[2], (DEPTH, D_MODEL, D_MODEL), jnp.float32) * (D_MODEL ** -0.5) * DEEPNORM_BETA
    rel_bias = jax.random.normal(ks[3], (DEPTH, N_HEADS_A, 2 * REL_CLIP + 1), jnp.float32) * 0.1
    lambda_q1 = jax.random.normal(ks[4], (DEPTH, HEAD_DIM), jnp.float32) * 0.1
    lambda_k1 = jax.random.normal(ks[5], (DEPTH, HEAD_DIM), jnp.float32) * 0.1
    lambda_q2 = jax.random.normal(ks[6], (DEPTH, HEAD_DIM), jnp.float32) * 0.1
    lambda_k2 = jax.random.normal(ks[7], (DEPTH, HEAD_DIM), jnp.float32) * 0.1
    subln_g = 1.0 + 0.02 * jax.random.normal(ks[8], (DEPTH, 2 * HEAD_DIM), jnp.float32)
    ln_g = 1.0 + 0.02 * jax.random.normal(ks[9], (DEPTH, D_MODEL), jnp.float32)
    ln_b = 0.02 * jax.random.normal(ks[10], (DEPTH, D_MODEL), jnp.float32)
    return {"x": x, "w_in": w_in, "w_out": w_out, "rel_bias": rel_bias,
            "lambda_q1": lambda_q1, "lambda_k1": lambda_k1, "lambda_q2": lambda_q2, "lambda_k2": lambda_k2,
            "subln_g": subln_g, "ln_g": ln_g, "ln_b": ln_b}


def reference(x, w_in, w_out, rel_bias, lambda_q1, lambda_k1, lambda_q2, lambda_k2, subln_g, ln_g, ln_b):
    for l in range(DEPTH):
        x = hybrid_layer(x, w_in[l], w_out[l], rel_bias[l], lambda_q1[l], lambda_k1[l],
                         lambda_q2[l], lambda_k2[l], subln_g[l], ln_g[l], ln_b[l], l)
    return x
```

```python
import math
import contextlib
import numpy as np
import ml_dtypes
import concourse.bass as bass
import concourse.mybir as mybir
from concourse.bass_utils import run_bass_kernel_spmd

F32 = mybir.dt.float32
BF16 = mybir.dt.bfloat16
AF = mybir.ActivationFunctionType
ALU = mybir.AluOpType
AX = mybir.AxisListType

ENG = ("pe", "act", "dve", "pool", "sp")


class Op:
    __slots__ = ("eng", "fn", "deps", "sig", "tok", "dma", "idx", "name")

    def __init__(self, eng, fn, dma, name):
        self.eng = eng
        self.fn = fn
        self.deps = []
        self.sig = False
        self.tok = None
        self.dma = dma
        self.name = name


class Tracker:
    def __init__(self, n_dma_sems=8):
        self.ops = []
        self.last_w = {}
        self.readers = {}
        self.n_dma_sems = n_dma_sems
        self.bar_from = 0
        self.pending_bar = {}

    def op(self, eng, fn, reads=(), writes=(), dma=False, name=""):
        o = Op(eng, fn, dma, name)
        o.idx = len(self.ops)
        deps = set()
        for b in reads:
            w = self.last_w.get(b)
            if w is not None:
                deps.add(w)
        for b in writes:
            w = self.last_w.get(b)
            if w is not None:
                deps.add(w)
            for r in self.readers.get(b, ()):
                deps.add(r)
        for b in reads:
            self.readers.setdefault(b, []).append(o.idx)
        for b in writes:
            self.last_w[b] = o.idx
            self.readers[b] = []
        pb = self.pending_bar.get(eng)
        if pb:
            deps |= pb
            self.pending_bar[eng] = None
        deps.discard(o.idx)
        o.deps = sorted(deps)
        self.ops.append(o)
        return o

    def barrier(self):
        deps = set()
        for e in ENG:
            for o in reversed(self.ops):
                if o.eng == e and not o.dma:
                    deps.add(o.idx)
                    break
        for o in self.ops[self.bar_from:]:
            if o.dma:
                deps.add(o.idx)
        self.bar_from = len(self.ops)
        self.pending_bar = {e: set(deps) for e in ENG}

    def emit(self, nc, final_wait_ops=()):
        ops = self.ops
        for o in ops:
            for d in o.deps:
                p = ops[d]
                if p.dma:
                    continue
                if p.eng == "pe" and o.eng == "pe" and not o.dma:
                    continue
                p.sig = True
        for i in final_wait_ops:
            if not ops[i].dma:
                ops[i].sig = True
        ticks = {e: 0 for e in ENG}
        dma_cnt = {e: 0 for e in ENG}
        per_eng = {e: [] for e in ENG}
        for o in ops:
            per_eng[o.eng].append(o)
        import contextlib
        with contextlib.ExitStack() as st:
            esem = {e: st.enter_context(nc.semaphore("s_" + e)) for e in ENG}
            dsem = {}
            for e in ENG:
                if any(o.dma for o in per_eng[e]):
                    dsem[e] = [st.enter_context(nc.semaphore("d_%s%d" % (e, k)))
                               for k in range(self.n_dma_sems)]
            for o in ops:
                if o.dma:
                    k = dma_cnt[o.eng]
                    dma_cnt[o.eng] += 1
                    s = k % self.n_dma_sems
                    o.tok = (dsem[o.eng][s], 16 * (k // self.n_dma_sems + 1), ("d", o.eng, s))
                elif o.sig:
                    ticks[o.eng] += 1
                    o.tok = (esem[o.eng], ticks[o.eng], ("e", o.eng))
            block = st.enter_context(nc.Block())

            def run(e, eng):
                known = {}
                dma_prev = {}
                nd = 0
                for o in per_eng[e]:
                    waits = []
                    for d in o.deps:
                        p = ops[d]
                        if p.tok is None:
                            continue
                        if (not p.dma) and p.eng == "pe" and e == "pe" and not o.dma:
                            continue
                        waits.append(p.tok)
                    if o.dma:
                        s = nd % self.n_dma_sems
                        nd += 1
                        prev = dma_prev.get(s)
                        if prev is not None:
                            waits.append(prev)
                        dma_prev[s] = o.tok
                    best = {}
                    for (sem, val, key) in waits:
                        if known.get(key, 0) >= val:
                            continue
                        if key not in best or best[key][1] < val:
                            best[key] = (sem, val)
                    for key, (sem, val) in best.items():
                        eng.wait_ge(sem, val)
                        known[key] = val
                    ins = o.fn(eng)
                    if o.tok is not None:
                        ins.then_inc(o.tok[0], 16 if o.dma else 1)
                if e == "sp":
                    for i in final_wait_ops:
                        sem, val, key = ops[i].tok
                        if known.get(key, 0) < val:
                            eng.wait_ge(sem, val)
                            known[key] = val

            @block.tensor
            def _(eng):
                run("pe", eng)

            @block.scalar
            def _(eng):
                run("act", eng)

            @block.vector
            def _(eng):
                run("dve", eng)

            @block.gpsimd
            def _(eng):
                run("pool", eng)

            @block.sync
            def _(eng):
                run("sp", eng)

D = 1024
ALPHA = (2 * 2) ** 0.25
LN_EPS = 1e-5
RMS_EPS = 1e-5
F_QA, F_KA, F_VA, F_GA, F_QB, F_KB, F_VB, F_GB, F_QBR, F_KBR = range(10)
NG = 24


def build(S, L=2):
    NBLK = S // 512
    nc = bass.Bass("TRN2", target_bir_lowering=False)
    dt_in = lambda n, s, d=F32: nc.dram_tensor(n, s, d, kind="ExternalInput").ap()
    x_in = dt_in("x", [S, D])
    w_in = dt_in("w_in", [L, D, 4096])
    w_out = dt_in("w_out", [L, D, D])
    rel_bias = dt_in("rel_bias", [L, 8, 257])
    lq1 = dt_in("lambda_q1", [L, 64]); lk1 = dt_in("lambda_k1", [L, 64])
    lq2 = dt_in("lambda_q2", [L, 64]); lk2 = dt_in("lambda_k2", [L, 64])
    subln = dt_in("subln_g", [L, 128])
    ln_g = dt_in("ln_g", [L, D]); ln_b = dt_in("ln_b", [L, D])
    c_ident = dt_in("c_ident", [128, 128], BF16)
    c_antij = dt_in("c_antij", [128, 128])
    c_cs = dt_in("c_cs", [2, 128, S])
    c_valid = dt_in("c_valid", [128, 640], BF16)
    out = nc.dram_tensor("out", [S, D], F32, kind="ExternalOutput").ap()
    wbf = nc.dram_tensor("wbf", [L, NG, 128, 8, 256], BF16).ap()
    x1s = nc.dram_tensor("x1s", [S, D], F32).ap()
    bextd = nc.dram_tensor("bextd", [8, 768], F32).ap()

    T = Tracker()
    st = contextlib.ExitStack()
    with st:
        sb = lambda n, s, d: st.enter_context(nc.sbuf_tensor(n, s, d))
        wbuf = [sb("wbuf%d" % i, [128, 8, 256], BF16) for i in range(4)]
        KB = sb("KB", [128, 4, S], BF16)
        VB = sb("VB", [128, S // 128, 512], BF16)
        KAr = sb("KAr", [128, 4, 1024], BF16)
        VAr = sb("VAr", [128, 8, 512], BF16)
        G = sb("G", [128, 8, 640], BF16)
        xT = sb("xT", [128, 8, 512], BF16)
        QA = sb("QA", [128, 8, 512], BF16)
        QB = sb("QB", [128, 8, 512], BF16)
        gates = sb("gates", [128, 8, 512], BF16)
        yT = sb("yT", [128, 8, 512], BF16)
        cs = sb("cs", [128, 2, 512], F32)
        PT = [sb("PT%d" % i, [128, 512], BF16) for i in range(4)]
        EB = [sb("EB%d" % i, [128, 512], BF16) for i in range(2)]
        xin2 = [sb("xin%d" % i, [128, 1024], F32) for i in range(2)]
        xb2 = [sb("xb%d" % i, [128, 1024], BF16) for i in range(2)]
        zz = sb("zz", [128, 2, 1024], F32)
        lng = sb("lng", [128, 1024], F32)
        lnb = sb("lnb", [128, 1024], F32)
        btmp = sb("btmp", [128, 4, 512], F32)
        recA = sb("recA", [128, 512], F32)
        tA = sb("tA", [128, 512], F32)
        rt1 = sb("rt1", [128, 512], F32)
        rt2 = sb("rt2", [128, 512], F32)
        junk = sb("junk", [128, 1024], BF16)
        ident = sb("ident", [128, 128], BF16)
        antij = sb("antij", [128, 128], F32)
        ones_bf = sb("ones_bf", [128, 128], BF16)
        ones_f = sb("ones_f", [128, 128], F32)
        validA = sb("validA", [128, 640], BF16)
        lamt = sb("lamt", [128, 4, 64], F32)
        sm = sb("sm", [128, 32], F32)
        pT = st.enter_context(nc.psum_tensor("pT", [128, 8, 128], BF16))
        gbk = [st.enter_context(nc.psum_tensor("gbk%d" % i, [128, 512], F32)) for i in range(3)]
        acc = [st.enter_context(nc.psum_tensor("acc%d" % i, [128, 512], F32)) for i in range(4)]

        C_S1, C_S2, C_E1, C_E2, C_LAM, C_NLAM, C_GSUB, C_SUM, C_SS, C_MEAN, C_VAR, C_RSTD, C_NB, C_MSQ = range(14)
        smc = lambda c: sm[:, c:c + 1]

        gctr = [0]

        def gen_bank():
            i = gctr[0] % 3
            gctr[0] += 1
            return i

        wctr = [0]

        def wslot():
            i = wctr[0] % 4
            wctr[0] += 1
            return i

        pctr = [0]

        def pslot():
            i = pctr[0] % 4
            pctr[0] += 1
            return i

        ectr = [0]

        def eslot():
            i = ectr[0] % 2
            ectr[0] += 1
            return i

        T.op("sp", lambda e: e.dma_start(out=ident[:], in_=c_ident[:, :]), writes=["ident"], dma=True)
        T.op("sp", lambda e: e.dma_start(out=antij[:], in_=c_antij[:, :]), writes=["antij"], dma=True)
        T.op("sp", lambda e: e.dma_start(out=validA[:], in_=c_valid[:, :]), writes=["validA"], dma=True)
        T.op("pool", lambda e: e.memset(ones_bf[:], 1.0), writes=["ones_bf"])
        T.op("pool", lambda e: e.memset(ones_f[:], 1.0 / 128.0), writes=["ones_f"])
        T.op("pool", lambda e: e.memset(QA[:], 0.0), writes=[("QA", i) for i in range(8)])
        T.op("pool", lambda e: e.memset(QB[:], 0.0), writes=[("QB", i) for i in range(8)])

        stg = [zz[:].rearrange("p a (c n) -> p (a c) n", n=256), btmp[:].rearrange("p a (c n) -> p (a c) n", n=256)]
        stg_key = [[("z", 0), ("z", 1)], ["o1", "o2", "r1", "r2"]]
        conv_order = [2, 3, 4, 5, 12, 13, 10, 11, 8, 9, 0, 1, 6, 7, 14, 15, 20, 21, 22, 23]
        k = 0
        OVERLAP = (S // 128 >= 32 and L == 2)
        for l in ([0] if OVERLAP else range(L)):
            for gid in conv_order:
                if gid < 16:
                    f, hf = gid // 2, gid % 2
                    src = w_in[l].rearrange("(c p) n -> p c n", p=128)[:, :, f * 512 + hf * 256: f * 512 + hf * 256 + 256]
                    scale = 0.125 if f in (F_QA, F_QB) else 1.0
                elif gid < 20:
                    continue
                else:
                    gg = gid - 20
                    src = w_out[l].rearrange("(c p) n -> p c n", p=128)[:, :, gg * 256: gg * 256 + 256]
                    scale = 1.0
                    f = -1
                sg = stg[k % 2]; sk = stg_key[k % 2]
                k += 1
                T.op("sp", lambda e, sg=sg, src=src: e.dma_start(out=sg, in_=src), writes=sk, dma=True)
                ws = wslot()
                ceng = "dve" if (k % 2) else "act"
                if ceng == "dve":
                    T.op("dve", lambda e, ws=ws, sg=sg, scale=scale: e.tensor_scalar(
                        out=wbuf[ws][:], in0=sg, scalar1=scale, scalar2=None, op0=ALU.mult),
                        reads=sk, writes=[("w", ws)])
                else:
                    T.op("act", lambda e, ws=ws, sg=sg, scale=scale: e.activation(
                        out=wbuf[ws][:], in_=sg, func=AF.Identity, scale=scale),
                        reads=sk, writes=[("w", ws)])
                T.op("pool", lambda e, ws=ws, l=l, gid=gid: e.dma_start(out=wbf[l, gid], in_=wbuf[ws][:]),
                     reads=[("w", ws)], writes=[("wbf", l, gid)], dma=True)
                if f in (F_QB, F_KB):
                    rgid = (16 if f == F_QB else 18) + hf
                    ws2 = wslot()
                    sv = sg.rearrange("p c (m t d) -> p (c m) t d", t=2, d=32)
                    dv = wbuf[ws2][:].rearrange("p c (m t d) -> p (c m) t d", t=2, d=32)

                    def rotf(e, sv=sv, dv=dv, scale=scale):
                        e.tensor_scalar(out=dv[:, :, 0, :], in0=sv[:, :, 1, :], scalar1=-scale, scalar2=None, op0=ALU.mult)
                        return e.tensor_scalar(out=dv[:, :, 1, :], in0=sv[:, :, 0, :], scalar1=scale, scalar2=None, op0=ALU.mult)

                    def rotf_act(e, sv=sv, dv=dv, scale=scale):
                        e.activation(out=dv[:, :, 0, :], in_=sv[:, :, 1, :], func=AF.Identity, scale=-scale)
                        return e.activation(out=dv[:, :, 1, :], in_=sv[:, :, 0, :], func=AF.Identity, scale=scale)
                    if ceng == "act":
                        T.op("dve", rotf, reads=sk, writes=[("w", ws2)])
                    else:
                        T.op("act", rotf_act, reads=sk, writes=[("w", ws2)])
                    T.op("pool", lambda e, ws2=ws2, l=l, rgid=rgid: e.dma_start(out=wbf[l, rgid], in_=wbuf[ws2][:]),
                         reads=[("w", ws2)], writes=[("wbf", l, rgid)], dma=True)
        out_ops = []
        conv = {"k": 0, "pend": None, "o": 0}
        if OVERLAP:
            cjobs = []
            for gid in list(range(16)) + list(range(20, 24)):
                if gid < 16:
                    f, hf = gid // 2, gid % 2
                    src = w_in[1].rearrange("(c p) n -> p c n", p=128)[:, :, f * 512 + hf * 256: f * 512 + hf * 256 + 256]
                    scale = 0.125 if f in (F_QA, F_QB) else 1.0
                else:
                    gg = gid - 20
                    src = w_out[1].rearrange("(c p) n -> p c n", p=128)[:, :, gg * 256: gg * 256 + 256]
                    scale = 1.0
                cjobs.append((gid, src, scale))
            c_stg = VB[:, 16:24, :].bitcast(F32)
            c_out = [VB[:, 24:28, :].rearrange("p t (a n) -> p (t a) n", n=256), VB[:, 28:32, :].rearrange("p t (a n) -> p (t a) n", n=256)]
            k_stg = [("VB", ti, hf) for ti in range(16, 24) for hf in range(2)]
            k_out = [[("VB", ti, hf) for ti in range(24, 28) for hf in range(2)], [("VB", ti, hf) for ti in range(28, 32) for hf in range(2)]]

        def conv_tick():
            if not OVERLAP:
                return
            if conv["pend"] is not None:
                gid, src, scale = conv["pend"]
                o = conv["o"] % 2
                conv["o"] += 1
                T.op("dve", lambda e, o=o, scale=scale: e.tensor_scalar(out=c_out[o], in0=c_stg, scalar1=scale, scalar2=None, op0=ALU.mult),
                     reads=k_stg, writes=k_out[o])
                T.op("pool", lambda e, o=o, gid=gid: e.dma_start(out=wbf[1, gid], in_=c_out[o]),
                     reads=k_out[o], writes=[("wbf", 1, gid)], dma=True)
                if gid < 16 and (gid // 2) in (F_QB, F_KB):
                    rgid = (16 if gid // 2 == F_QB else 18) + gid % 2
                    o2 = conv["o"] % 2
                    conv["o"] += 1
                    sv = c_stg.rearrange("p c (m t d) -> p (c m) t d", t=2, d=32)
                    dv = c_out[o2].rearrange("p c (m t d) -> p (c m) t d", t=2, d=32)

                    def rotf2(e, sv=sv, dv=dv, scale=scale):
                        e.tensor_scalar(out=dv[:, :, 0, :], in0=sv[:, :, 1, :], scalar1=-scale, scalar2=None, op0=ALU.mult)
                        return e.tensor_scalar(out=dv[:, :, 1, :], in0=sv[:, :, 0, :], scalar1=scale, scalar2=None, op0=ALU.mult)
                    T.op("dve", rotf2, reads=k_stg, writes=k_out[o2])
                    T.op("pool", lambda e, o2=o2, rgid=rgid: e.dma_start(out=wbf[1, rgid], in_=c_out[o2]),
                         reads=k_out[o2], writes=[("wbf", 1, rgid)], dma=True)
                conv["pend"] = None
            if conv["k"] < len(cjobs):
                job = cjobs[conv["k"]]
                conv["k"] += 1
                T.op("pool", lambda e, src=job[1]: e.dma_start(out=c_stg, in_=src), writes=k_stg, dma=True)
                conv["pend"] = job
        def do_layer(l):
            lam_init = 0.8 - 0.6 * math.exp(-0.3 * l)
            src_x = x_in if l == 0 else x1s
            dst_x = out if l == L - 1 else x1s
            src_key = "xin_d" if l == 0 else "x1s"
            dst_key = "out_d" if l == L - 1 else "x1s"

            for i, v in enumerate((lq1, lk1, lq2, lk2)):
                bsrc = bass.AP(v.tensor, l * 64, [[0, 128], [1, 64]])
                T.op("sp", lambda e, i=i, bsrc=bsrc: e.dma_start(out=lamt[:, i, :], in_=bsrc), writes=[("lamt", i)], dma=True)
            T.op("dve", lambda e: e.tensor_tensor(out=lamt[:, 0, :], in0=lamt[:, 0, :], in1=lamt[:, 1, :], op=ALU.mult),
                 reads=[("lamt", 0), ("lamt", 1)], writes=[("lamt", 0)])
            T.op("dve", lambda e: e.tensor_tensor(out=lamt[:, 2, :], in0=lamt[:, 2, :], in1=lamt[:, 3, :], op=ALU.mult),
                 reads=[("lamt", 2), ("lamt", 3)], writes=[("lamt", 2)])
            T.op("dve", lambda e: e.reduce_sum(out=smc(C_S1), in_=lamt[:, 0, :], axis=AX.X), reads=[("lamt", 0)], writes=["s1"])
            T.op("dve", lambda e: e.reduce_sum(out=smc(C_S2), in_=lamt[:, 2, :], axis=AX.X), reads=[("lamt", 2)], writes=["s2"])
            T.op("act", lambda e: e.activation(out=smc(C_E1), in_=smc(C_S1), func=AF.Exp), reads=["s1"], writes=["e1"])
            T.op("act", lambda e: e.activation(out=smc(C_E2), in_=smc(C_S2), func=AF.Exp), reads=["s2"], writes=["e2"])
            T.op("dve", lambda e: e.tensor_tensor(out=smc(C_LAM), in0=smc(C_E1), in1=smc(C_E2), op=ALU.subtract),
                 reads=["e1", "e2"], writes=["lam"])
            T.op("dve", lambda e, li=lam_init: e.tensor_scalar(out=smc(C_NLAM), in0=smc(C_LAM), scalar1=-1.0, scalar2=-li,
                                                             op0=ALU.mult, op1=ALU.add), reads=["lam"], writes=["nlam"])
            gsrc = bass.AP(subln.tensor, l * 128, [[1, 128], [1, 1]])
            T.op("sp", lambda e, gsrc=gsrc: e.dma_start(out=smc(C_GSUB), in_=gsrc), writes=["gsub0"], dma=True)
            T.op("dve", lambda e, li=lam_init: e.tensor_scalar(out=smc(C_GSUB), in0=smc(C_GSUB), scalar1=1.0 - li, scalar2=None,
                                                             op0=ALU.mult), reads=["gsub0"], writes=["gsub"])
            T.op("sp", lambda e, l=l: e.dma_start(out=lng[:], in_=bass.AP(ln_g.tensor, l * D, [[0, 128], [1, D]])), writes=["lng"], dma=True)
            T.op("sp", lambda e, l=l: e.dma_start(out=lnb[:], in_=bass.AP(ln_b.tensor, l * D, [[0, 128], [1, D]])), writes=["lnb"], dma=True)
            def build_G():
                rb = btmp[0:8, 0, 0:257]
                bx = btmp[0:8, 1, :]
                bx2 = btmp[0:8, 2, 0:256]
                zr = btmp[0:8, 3, :]
                T.op("sp", lambda e, l=l: e.dma_start(out=rb, in_=rel_bias[l]), writes=["o1"], dma=True)
                T.op("pool", lambda e: e.memset(zr, 0.0), writes=["r2"])
                T.op("dve", lambda e: e.tensor_copy(out=bx[:, 0:256], in_=btmp[0:8, 0, 1:257]), reads=["o1"], writes=["o2"])
                T.op("dve", lambda e: e.tensor_scalar(out=bx[:, 256:512], in0=zr[:, 0:256], scalar1=btmp[0:8, 0, 256:257], scalar2=None,
                                                      op0=ALU.add), reads=["o1", "r2"], writes=["o2"])
                T.op("dve", lambda e: e.tensor_scalar(out=bx2, in0=zr[:, 0:256], scalar1=btmp[0:8, 0, 256:257], scalar2=None,
                                                      op0=ALU.add), reads=["o1", "r2"], writes=["r1"])
                T.op("pool", lambda e: e.dma_start(out=bextd[:, 0:512], in_=bx), reads=["o2", "o2"], writes=["bextd0"], dma=True)
                T.op("pool", lambda e: e.dma_start(out=bextd[:, 512:768], in_=bx2), reads=["r1"], writes=["bextd1"], dma=True)
                for h in range(8):
                    hk = zz[:, 0, 0:640]
                    ex = zz[:, 1, 0:640]
                    hsrc = bass.AP(bextd.tensor, h * 768, [[1, 128], [1, 640]])
                    T.op("sp", lambda e, hsrc=hsrc, hk=hk: e.dma_start(out=hk, in_=hsrc), reads=["bextd0", "bextd1"], writes=[("z", 0)], dma=True)
                    b0, b1 = gen_bank(), gen_bank()

                    def mj(e, b0=b0, b1=b1, hk=hk):
                        e.matmul(gbk[b0][:, 0:512], lhsT=antij[:], rhs=hk[:, 0:512], start=True, stop=True)
                        return e.matmul(gbk[b1][:, 0:128], lhsT=antij[:], rhs=hk[:, 512:640], start=True, stop=True)
                    T.op("pe", mj, reads=[("z", 0), "antij"], writes=[("g", b0), ("g", b1)])
                    T.op("act", lambda e, b0=b0, ex=ex: e.activation(out=ex[:, 0:512], in_=gbk[b0][:, 0:512], func=AF.Exp),
                         reads=[("g", b0)], writes=[("z", 1)])
                    T.op("act", lambda e, b1=b1, ex=ex: e.activation(out=ex[:, 512:640], in_=gbk[b1][:, 0:128], func=AF.Exp),
                         reads=[("g", b1)], writes=[("z", 1)])
                    T.op("dve", lambda e, h=h, ex=ex: e.tensor_tensor(out=G[:, h, :], in0=ex, in1=validA[:], op=ALU.mult),
                         reads=[("z", 1), ("z", 1), "validA"], writes=[("G", h)])

            def stage_T(j):
                def ld(i):
                    r0 = j * 512 + i * 128
                    xin = xin2[i % 2]
                    T.op("sp", lambda e: e.dma_start(out=xin[:], in_=src_x[r0:r0 + 128, :]),
                         reads=[(src_key, r0 // 128)], writes=[("xin", i % 2)], dma=True)
                ld(0)
                ld(1)
                for i in range(4):
                    xin = xin2[i % 2]; xb = xb2[i % 2]
                    T.op("dve", lambda e, xin=xin, xb=xb: e.tensor_copy(out=xb[:], in_=xin[:]), reads=[("xin", i % 2)], writes=[("xb", i % 2)])
                    if i + 2 < 4:
                        ld(i + 2)

                    def tr(e, xb=xb):
                        for c in range(8):
                            ins = e.transpose(out=pT[:, c, :], in_=xb[:, c * 128:(c + 1) * 128], identity=ident[:])
                        return ins
                    T.op("pe", tr, reads=[("xb", i % 2), "ident"], writes=["pT"])
                    T.op("act" if i % 2 else "dve",
                         (lambda e, i=i: e.copy(out=xT[:, :, i * 128:(i + 1) * 128], in_=pT[:])) if i % 2 else
                         (lambda e, i=i: e.tensor_copy(out=xT[:, :, i * 128:(i + 1) * 128], in_=pT[:])),
                         reads=["pT"], writes=[("xT", i)])
                T.op("sp", lambda e, j=j: e.dma_start(out=cs[:, 0, :], in_=c_cs[0, :, j * 512:(j + 1) * 512]), writes=[("cs", 0)], dma=True)
                T.op("sp", lambda e, j=j: e.dma_start(out=cs[:, 1, :], in_=c_cs[1, :, j * 512:(j + 1) * 512]), writes=[("cs", 1)], dma=True)

            xT_keys = [("xT", i) for i in range(4)]

            def load_w(gid):
                ws = wslot()
                T.op("sp", lambda e, ws=ws, gid=gid: e.dma_start(out=wbuf[ws][:], in_=wbf[l, gid]),
                     reads=[("wbf", l, gid)], writes=[("w", ws)], dma=True)
                return ws

            def fm_tile(ws, m, b):
                def f(e):
                    for c in range(8):
                        ins = e.matmul(gbk[b][:, :], lhsT=wbuf[ws][:, c, m * 128:(m + 1) * 128], rhs=xT[:, c, :],
                                       start=(c == 0), stop=(c == 7))
                    return ins
                T.op("pe", f, reads=[("w", ws)] + xT_keys, writes=[("g", b)])

            def tm_tile(ws, i, b):
                def f(e):
                    for c in range(8):
                        ins = e.matmul(gbk[b][:, 0:256], lhsT=xT[:, c, i * 128:(i + 1) * 128], rhs=wbuf[ws][:, c, :],
                                       start=(c == 0), stop=(c == 7))
                    return ins
                T.op("pe", f, reads=[("w", ws), ("xT", i)], writes=[("g", b)])

            def stage_P(j):
                slot = j % 2
                t0 = j * 512
                ev = [0]

                def cp_eng():
                    ev[0] += 1
                    return "act" if ev[0] % 2 else "dve"

                def copy_op(eng, o, i_, reads, writes):
                    if eng == "act":
                        T.op("act", lambda e: e.copy(out=o, in_=i_), reads=reads, writes=writes)
                    else:
                        T.op("dve", lambda e: e.tensor_copy(out=o, in_=i_), reads=reads, writes=writes)
                for hf in range(2):
                    ws = load_w(2 * F_KA + hf)
                    for m in range(2):
                        hp = 2 * hf + m
                        b = gen_bank()
                        fm_tile(ws, m, b)
                        copy_op(cp_eng(), KAr[:, hp, slot * 512:(slot + 1) * 512], gbk[b][:, :], [("g", b)], [("KAr", slot, hp)])
                for f_, dstT, dkey in ((F_VA, VAr, "VAr"), (F_VB, VB, "VB")):
                    for hf in range(2):
                        ws = load_w(2 * f_ + hf)
                        for i in range(4):
                            b = gen_bank()
                            tm_tile(ws, i, b)
                            ti = (slot * 4 + i) if f_ == F_VA else (j * 4 + i)
                            copy_op(cp_eng(), dstT[:, ti, hf * 256:(hf + 1) * 256], gbk[b][:, 0:256], [("g", b)], [(dkey, ti, hf)])
                for f_, fr_ in ((F_KB, F_KBR), (F_QB, F_QBR)):
                    for hf in range(2):
                        ws = load_w(2 * f_ + hf)
                        ws2 = load_w(2 * fr_ + hf)
                        for m in range(2):
                            h = 2 * hf + m
                            b = gen_bank(); b2 = gen_bank()
                            fm_tile(ws, m, b)
                            fm_tile(ws2, m, b2)
                            T.op("dve", lambda e, b=b: e.tensor_tensor(out=rt1[:], in0=gbk[b][:, :], in1=cs[:, 0, :], op=ALU.mult),
                                 reads=[("g", b), ("cs", 0)], writes=["rt1"])
                            T.op("dve", lambda e, b2=b2: e.tensor_tensor(out=rt2[:], in0=gbk[b2][:, :], in1=cs[:, 1, :], op=ALU.mult),
                                 reads=[("g", b2), ("cs", 1)], writes=["rt2"])
                            if f_ == F_KB:
                                T.op("dve", lambda e, h=h: e.tensor_tensor(out=KB[:, h, t0:t0 + 512], in0=rt1[:], in1=rt2[:], op=ALU.add),
                                     reads=["rt1", "rt2"], writes=[("KB", h, j)])
                            else:
                                def qadd(e, h=h):
                                    e.tensor_tensor(out=QB[0:64, 2 * h, :], in0=rt1[0:64, :], in1=rt2[0:64, :], op=ALU.add)
                                    return e.tensor_tensor(out=QB[64:128, 2 * h + 1, :], in0=rt1[64:128, :], in1=rt2[64:128, :], op=ALU.add)
                                T.op("dve", qadd, reads=["rt1", "rt2"], writes=[("QB", 2 * h), ("QB", 2 * h + 1)])
                for hf in range(2):
                    ws = load_w(2 * F_QA + hf)
                    for m in range(2):
                        hp = 2 * hf + m
                        b = gen_bank()
                        fm_tile(ws, m, b)
                        copy_op("act", QA[0:64, 2 * hp, :], gbk[b][0:64, :], [("g", b)], [("QA", 2 * hp)])
                        copy_op("dve", QA[64:128, 2 * hp + 1, :], gbk[b][64:128, :], [("g", b)], [("QA", 2 * hp + 1)])
                for f_, off in ((F_GA, 0), (F_GB, 4)):
                    for hf in range(2):
                        ws = load_w(2 * f_ + hf)
                        for m in range(2):
                            ch = off + 2 * hf + m
                            b = gen_bank()
                            fm_tile(ws, m, b)
                            T.op("act", lambda e, b=b, ch=ch: e.activation(out=gates[:, ch, :], in_=gbk[b][:, :], func=AF.Silu),
                                 reads=[("g", b)], writes=[("gates", ch)])

            def pipeline(units, la=3):
                n = len(units)
                for i in range(n + la):
                    if i < n:
                        units[i][0]()
                    if i - la >= 0:
                        u = units[i - la]
                        u[1]()
                        if len(u) > 2:
                            u[2]()

            def stage_A(j):
                us = []
                for h in range(8):
                    us += head_A(j, h)
                return us

            def head_A(j, h):
                if True:
                    hp, par = h // 2, h % 2
                    ob, db = (0, 1) if par == 0 else (2, 3)
                    rows = slice(0, 64) if par == 0 else slice(64, 128)
                    rs = [r for r in range(8) if not (j == 0 and r < 4)]
                    units = []
                    for ui, r in enumerate(rs):
                        if r < 4:
                            kslot, ktile = (j - 1) % 2, r
                            q0, N, u0 = 0, 128 * (r + 1), 512 - 128 * r
                        else:
                            kslot, ktile = j % 2, r - 4
                            q0 = 128 * (r - 4); N = 512 - q0; u0 = 0
                        st_ = {}

                        def sc(r=r, kslot=kslot, ktile=ktile, q0=q0, N=N, u0=u0, st_=st_):
                            b = gen_bank(); es = eslot(); ps = pslot()
                            st_["ps"] = ps
                            T.op("pe", lambda e: e.matmul(gbk[b][:, 0:N], lhsT=KAr[:, hp, kslot * 512 + ktile * 128: kslot * 512 + ktile * 128 + 128],
                                                          rhs=QA[:, h, q0:q0 + N], start=True, stop=True),
                                 reads=[("KAr", kslot, hp), ("QA", h)], writes=[("g", b)])
                            T.op("act", lambda e: e.activation(out=EB[es][:, 0:N], in_=gbk[b][:, 0:N], func=AF.Exp),
                                 reads=[("g", b)], writes=[("EB", es)])
                            T.op("dve", lambda e: e.tensor_tensor(out=PT[ps][:, 0:N], in0=EB[es][:, 0:N], in1=G[:, h, u0:u0 + N], op=ALU.mult),
                                 reads=[("EB", es), ("G", h)], writes=[("PT", ps)])

                        def pv(ui=ui, kslot=kslot, ktile=ktile, q0=q0, N=N, st_=st_, last=(ui == len(rs) - 1)):
                            ps = st_["ps"]

                            def f(e):
                                e.matmul(acc[ob][:, q0:q0 + N], lhsT=VAr[:, kslot * 4 + ktile, hp * 128:(hp + 1) * 128], rhs=PT[ps][:, 0:N],
                                         start=(ui == 0), stop=last, skip_group_check=True)
                                return e.matmul(acc[db][:, q0:q0 + N], lhsT=ones_bf[:], rhs=PT[ps][:, 0:N],
                                                start=(ui == 0), stop=last, skip_group_check=True)
                            T.op("pe", f, reads=[("PT", ps), ("VAr", kslot * 4 + ktile, hp // 2), "ones_bf"],
                                 writes=[("acc", ob), ("acc", db)])
                        units.append((sc, pv))
                    def epi():
                        T.op("act", lambda e, db=db, rows=rows: e.activation(out=recA[rows, :], in_=acc[db][rows, :], func=AF.Ln),
                             reads=[("acc", db)], writes=[("recA", par)])
                        T.op("act", lambda e, rows=rows: e.activation(out=recA[rows, :], in_=recA[rows, :], func=AF.Exp, scale=-1.0),
                             reads=[("recA", par)], writes=[("recA", par)])
                        T.op("dve", lambda e, ob=ob, rows=rows: e.tensor_tensor(out=tA[rows, :], in0=acc[ob][rows, :], in1=recA[rows, :], op=ALU.mult),
                             reads=[("acc", ob), ("recA", par)], writes=[("tA", par)])
                        T.op("dve", lambda e, rows=rows, hp=hp: e.tensor_tensor(out=yT[rows, hp, :], in0=tA[rows, :], in1=gates[rows, hp, :], op=ALU.mult),
                             reads=[("tA", par), ("gates", hp)], writes=[("yT", hp, par)])
                    units[-1] = (units[-1][0], units[-1][1], epi)
                    return units

            def stage_B(j):
                us = []
                for h in range(4):
                    us += head_B(j, h)
                return us

            def head_B(j, h):
                if True:
                    kts = [(4 * j + a, 128 * a, True) for a in range(4)] + [(kt, 0, False) for kt in range(4 * j)]
                    units = []
                    nu = 2 * len(kts)
                    for ki, (kt, q0, diag) in enumerate(kts):
                        for n in range(2):
                            ui = 2 * ki + n
                            N = 512 - q0
                            st_ = {}

                            def sc(kt=kt, q0=q0, N=N, n=n, diag=diag, st_=st_):
                                b = gen_bank(); ps = pslot()
                                st_["ps"] = ps
                                T.op("pe", lambda e: e.matmul(gbk[b][:, 0:N], lhsT=KB[:, h, kt * 128:(kt + 1) * 128],
                                                              rhs=QB[:, 2 * h + n, q0:512], start=True, stop=True),
                                     reads=[("KB", h, kt // 4), ("QB", 2 * h + n)], writes=[("g", b)])
                                T.op("act", lambda e: e.activation(out=PT[ps][:, 0:N], in_=gbk[b][:, 0:N], func=AF.Exp),
                                     reads=[("g", b)], writes=[("PT", ps)])
                                if diag:
                                    T.op("pool", lambda e: e.memset(PT[ps][64:128, 0:64], 0.0), reads=[("PT", ps)], writes=[("PT", ps)])

                            def pv(kt=kt, q0=q0, N=N, n=n, st_=st_, first=(ki == 0), last=(ki == len(kts) - 1)):
                                ps = st_["ps"]

                                def f(e):
                                    e.matmul(acc[n][:, q0:512], lhsT=VB[:, kt, h * 128:(h + 1) * 128], rhs=PT[ps][:, 0:N],
                                             start=first, stop=last, skip_group_check=True)
                                    return e.matmul(acc[2 + n][:, q0:512], lhsT=ones_bf[:], rhs=PT[ps][:, 0:N],
                                                    start=first, stop=last, skip_group_check=True)
                                T.op("pe", f, reads=[("PT", ps), ("VB", kt, h // 2), "ones_bf"], writes=[("acc", n), ("acc", 2 + n)])
                            units.append((sc, pv))
                    def epi():
                        o1, o2, r1, r2 = btmp[:, 0, :], btmp[:, 1, :], btmp[:, 2, :], btmp[:, 3, :]
                        T.op("dve", lambda e: e.tensor_copy(out=r1, in_=acc[2][:, :]), reads=[("acc", 2)], writes=["r1"])
                        T.op("dve", lambda e: e.tensor_copy(out=o1, in_=acc[0][:, :]), reads=[("acc", 0)], writes=["o1"])
                        T.op("dve", lambda e: e.tensor_copy(out=r2, in_=acc[3][:, :]), reads=[("acc", 3)], writes=["r2"])
                        T.op("dve", lambda e: e.tensor_copy(out=o2, in_=acc[1][:, :]), reads=[("acc", 1)], writes=["o2"])
                        T.op("act", lambda e: e.activation(out=r1, in_=r1, func=AF.Ln), reads=["r1"], writes=["r1"])
                        T.op("act", lambda e: e.activation(out=r2, in_=r2, func=AF.Ln), reads=["r2"], writes=["r2"])
                        T.op("act", lambda e: e.activation(out=r1, in_=r1, func=AF.Exp, scale=-1.0), reads=["r1"], writes=["r1"])
                        T.op("act", lambda e: e.activation(out=r2, in_=r2, func=AF.Exp, scale=-1.0), reads=["r2"], writes=["r2"])
                        T.op("dve", lambda e: e.tensor_tensor(out=o1, in0=o1, in1=r1, op=ALU.mult), reads=["o1", "r1"], writes=["o1"])
                        T.op("dve", lambda e: e.tensor_tensor(out=o2, in0=o2, in1=r2, op=ALU.mult), reads=["o2", "r2"], writes=["o2"])
                        T.op("dve", lambda e: e.scalar_tensor_tensor(out=o1, in0=o2, scalar=smc(C_NLAM), in1=o1, op0=ALU.mult, op1=ALU.add),
                             reads=["o1", "o2", "nlam"], writes=["o1"])
                        T.op("dve", lambda e: e.tensor_tensor(out=r1, in0=o1, in1=o1, op=ALU.mult), reads=["o1"], writes=["r1"])
                        b = gen_bank()
                        T.op("pe", lambda e, b=b: e.matmul(gbk[b][:, :], lhsT=ones_f[:], rhs=r1, start=True, stop=True),
                             reads=["r1", "ones_f"], writes=[("g", b)])
                        T.op("act", lambda e, b=b: e.activation(out=r2, in_=gbk[b][:, :], func=AF.Ln, bias=RMS_EPS), reads=[("g", b)], writes=["r2"])
                        T.op("act", lambda e: e.activation(out=r2, in_=r2, func=AF.Exp, scale=-0.5), reads=["r2"], writes=["r2"])
                        T.op("dve", lambda e: e.tensor_tensor(out=o1, in0=o1, in1=r2, op=ALU.mult), reads=["o1", "r2"], writes=["o1"])
                        T.op("dve", lambda e, h=h: e.scalar_tensor_tensor(out=yT[:, 4 + h, :], in0=o1, scalar=smc(C_GSUB), in1=gates[:, 4 + h, :],
                                                                     op0=ALU.mult, op1=ALU.mult),
                             reads=["o1", "gsub", ("gates", 4 + h)], writes=[("yT", 4 + h, 0), ("yT", 4 + h, 1)])
                    units[-1] = (units[-1][0], units[-1][1], epi)
                    return units

            yT_keys = [("yT", c, p) for c in range(8) for p in range(2)]

            def stage_O(j):
                wss = [load_w(20 + gg) for gg in range(4)]
                for i in range(4):
                    r0 = j * 512 + i * 128
                    zb = i % 2
                    z = zz[:, zb, :]
                    T.op("sp", lambda e, r0=r0, z=z: e.dma_start(out=z, in_=src_x[r0:r0 + 128, :]),
                         reads=[(src_key, r0 // 128)], writes=[("z", zb)], dma=True)
                    for gg in range(4):
                        b = gen_bank()

                        def f(e, b=b, gg=gg, i=i):
                            for c in range(8):
                                ins = e.matmul(gbk[b][:, 0:256], lhsT=yT[:, c, i * 128:(i + 1) * 128], rhs=wbuf[wss[gg]][:, c, :],
                                               start=(c == 0), stop=(c == 7))
                            return ins
                        T.op("pe", f, reads=[("w", wss[gg])] + yT_keys, writes=[("g", b)])
                        T.op("dve", lambda e, b=b, gg=gg, z=z: e.scalar_tensor_tensor(
                            out=z[:, gg * 256:(gg + 1) * 256], in0=z[:, gg * 256:(gg + 1) * 256], scalar=ALPHA, in1=gbk[b][:, 0:256],
                            op0=ALU.mult, op1=ALU.add), reads=[("g", b), ("z", zb)], writes=[("z", zb)])
                    T.op("dve", lambda e, z=z: e.reduce_sum(out=smc(C_SUM), in_=z, axis=AX.X), reads=[("z", zb)], writes=["sum"])
                    T.op("act", lambda e, z=z: e.activation(out=junk[:], in_=z, func=AF.Square), reads=[("z", zb)], writes=["junk"])
                    T.op("dve", lambda e: e.reduce_sum(out=smc(C_SS), in_=junk[:], axis=AX.X), reads=["junk"], writes=["ss"])
                    T.op("dve", lambda e: e.tensor_scalar(out=smc(C_MEAN), in0=smc(C_SUM), scalar1=1.0 / D, scalar2=None, op0=ALU.mult),
                         reads=["sum"], writes=["mean"])
                    T.op("dve", lambda e: e.tensor_tensor(out=smc(C_MSQ), in0=smc(C_MEAN), in1=smc(C_MEAN), op=ALU.mult),
                         reads=["mean"], writes=["msq"])
                    T.op("dve", lambda e: e.scalar_tensor_tensor(out=smc(C_VAR), in0=smc(C_SS), scalar=1.0 / D, in1=smc(C_MSQ),
                                                                 op0=ALU.mult, op1=ALU.subtract), reads=["ss", "msq"], writes=["var"])
                    T.op("act", lambda e: e.activation(out=smc(C_RSTD), in_=smc(C_VAR), func=AF.Ln, bias=LN_EPS), reads=["var"], writes=["rstd"])
                    T.op("act", lambda e: e.activation(out=smc(C_RSTD), in_=smc(C_RSTD), func=AF.Exp, scale=-0.5), reads=["rstd"], writes=["rstd"])
                    T.op("dve", lambda e: e.scalar_tensor_tensor(out=smc(C_NB), in0=smc(C_MEAN), scalar=-1.0, in1=smc(C_RSTD),
                                                                 op0=ALU.mult, op1=ALU.mult), reads=["mean", "rstd"], writes=["nb"])
                    T.op("act", lambda e, z=z: e.activation(out=z, in_=z, func=AF.Identity, bias=smc(C_NB), scale=smc(C_RSTD)),
                         reads=[("z", zb), "nb", "rstd"], writes=[("z", zb)])
                    T.op("dve", lambda e, z=z: e.tensor_tensor(out=z, in0=z, in1=lng[:], op=ALU.mult), reads=[("z", zb), "lng"], writes=[("z", zb)])
                    T.op("dve", lambda e, z=z: e.tensor_tensor(out=z, in0=z, in1=lnb[:], op=ALU.add), reads=[("z", zb), "lnb"], writes=[("z", zb)])
                    o = T.op("pool", lambda e, r0=r0, z=z: e.dma_start(out=dst_x[r0:r0 + 128, :], in_=z),
                             reads=[("z", zb)], writes=[(dst_key, r0 // 128)], dma=True)
                    if l == L - 1:
                        out_ops.append(o.idx)

            stage_T(0)
            stage_P(0)
            build_G()
            for j in range(NBLK):
                if j + 1 < NBLK:
                    stage_T(j + 1)
                if l == 0 and j < 4:
                    us = stage_A(j)
                    us.append((conv_tick, lambda: None))
                    for h in range(4):
                        us += head_B(j, h)
                        us.append((conv_tick, lambda: None))
                    pipeline(us)
                    if j == 3:
                        while conv["pend"] is not None or (OVERLAP and conv["k"] < len(cjobs)):
                            conv_tick()
                else:
                    pipeline(stage_A(j) + stage_B(j))
                stage_O(j)
                if j + 1 < NBLK:
                    stage_P(j + 1)
        for l in range(L):
            do_layer(l)
        T.emit(nc, final_wait_ops=out_ops)
    return nc


def make_consts(S):
    ident = np.eye(128, dtype=np.float32).astype(ml_dtypes.bfloat16)
    antij = np.ascontiguousarray(np.eye(128, dtype=np.float32)[::-1])
    inv_freq = 10000.0 ** (-np.arange(0, 64, 2, dtype=np.float32) / 64.0)
    pos = np.arange(S, dtype=np.float32)
    ang = pos[None, :] * np.concatenate([inv_freq, inv_freq])[:, None]
    ang = np.concatenate([ang, ang], axis=0).astype(np.float32)
    cs = np.stack([np.cos(ang), np.sin(ang)]).astype(np.float32)
    kk = np.arange(128)[:, None] // 64
    uu = np.arange(640)[None, :] // 64
    valid = ((uu - kk >= 0) & (uu - kk <= 8)).astype(np.float32).astype(ml_dtypes.bfloat16)
    return {"c_ident": ident, "c_antij": antij, "c_cs": cs, "c_valid": valid}


_CACHE = {}


def kernel(x, w_in, w_out, rel_bias, lambda_q1, lambda_k1, lambda_q2, lambda_k2, subln_g, ln_g, ln_b):
    x = np.asarray(x, dtype=np.float32)
    B, S, _ = x.shape
    L = int(np.asarray(w_in).shape[0])
    key = (S, L)
    if key not in _CACHE:
        _CACHE[key] = build(S, L)
    nc = _CACHE[key]
    consts = make_consts(S)
    shared = {"w_in": np.ascontiguousarray(w_in, dtype=np.float32), "w_out": np.ascontiguousarray(w_out, dtype=np.float32),
              "rel_bias": np.ascontiguousarray(rel_bias, dtype=np.float32),
              "lambda_q1": np.ascontiguousarray(lambda_q1, dtype=np.float32), "lambda_k1": np.ascontiguousarray(lambda_k1, dtype=np.float32),
              "lambda_q2": np.ascontiguousarray(lambda_q2, dtype=np.float32), "lambda_k2": np.ascontiguousarray(lambda_k2, dtype=np.float32),
              "subln_g": np.ascontiguousarray(subln_g, dtype=np.float32),
              "ln_g": np.ascontiguousarray(ln_g, dtype=np.float32), "ln_b": np.ascontiguousarray(ln_b, dtype=np.float32)}
    shared.update(consts)
    in_maps = [dict(shared, x=np.ascontiguousarray(x[b])) for b in range(B)]
    res = run_bass_kernel_spmd(nc, in_maps, core_ids=list(range(B)))
    return np.stack([np.asarray(r["out"], dtype=np.float32) for r in res.results], axis=0)
```

```python
import math
import contextlib
import numpy as np
import ml_dtypes
import concourse.bass as bass
import concourse.mybir as mybir
from concourse.bass_utils import run_bass_kernel_spmd

F32 = mybir.dt.float32
BF16 = mybir.dt.bfloat16
AF = mybir.ActivationFunctionType
ALU = mybir.AluOpType
AX = mybir.AxisListType

ENG = ("pe", "act", "dve", "pool", "sp")


class Op:
    __slots__ = ("eng", "fn", "deps", "sig", "tok", "dma", "idx", "name")

    def __init__(self, eng, fn, dma, name):
        self.eng = eng
        self.fn = fn
        self.deps = []
        self.sig = False
        self.tok = None
        self.dma = dma
        self.name = name


class Tracker:
    def __init__(self, n_dma_sems=8):
        self.ops = []
        self.last_w = {}
        self.readers = {}
        self.n_dma_sems = n_dma_sems
        self.bar_from = 0
        self.pending_bar = {}

    def op(self, eng, fn, reads=(), writes=(), dma=False, name=""):
        o = Op(eng, fn, dma, name)
        o.idx = len(self.ops)
        deps = set()
        for b in reads:
            w = self.last_w.get(b)
            if w is not None:
                deps.add(w)
        for b in writes:
            w = self.last_w.get(b)
            if w is not None:
                deps.add(w)
            for r in self.readers.get(b, ()):
                deps.add(r)
        for b in reads:
            self.readers.setdefault(b, []).append(o.idx)
        for b in writes:
            self.last_w[b] = o.idx
            self.readers[b] = []
        pb = self.pending_bar.get(eng)
        if pb:
            deps |= pb
            self.pending_bar[eng] = None
        deps.discard(o.idx)
        o.deps = sorted(deps)
        self.ops.append(o)
        return o

    def barrier(self):
        deps = set()
        for e in ENG:
            for o in reversed(self.ops):
                if o.eng == e and not o.dma:
                    deps.add(o.idx)
                    break
        for o in self.ops[self.bar_from:]:
            if o.dma:
                deps.add(o.idx)
        self.bar_from = len(self.ops)
        self.pending_bar = {e: set(deps) for e in ENG}

    def emit(self, nc, final_wait_ops=()):
        ops = self.ops
        for o in ops:
            for d in o.deps:
                p = ops[d]
                if p.dma:
                    continue
                if p.eng == "pe" and o.eng == "pe" and not o.dma:
                    continue
                p.sig = True
        for i in final_wait_ops:
            if not ops[i].dma:
                ops[i].sig = True
        ticks = {e: 0 for e in ENG}
        dma_cnt = {e: 0 for e in ENG}
        per_eng = {e: [] for e in ENG}
        for o in ops:
            per_eng[o.eng].append(o)
        import contextlib
        with contextlib.ExitStack() as st:
            esem = {e: st.enter_context(nc.semaphore("s_" + e)) for e in ENG}
            dsem = {}
            for e in ENG:
                if any(o.dma for o in per_eng[e]):
                    dsem[e] = [st.enter_context(nc.semaphore("d_%s%d" % (e, k)))
                               for k in range(self.n_dma_sems)]
            for o in ops:
                if o.dma:
                    k = dma_cnt[o.eng]
                    dma_cnt[o.eng] += 1
                    s = k % self.n_dma_sems
                    o.tok = (dsem[o.eng][s], 16 * (k // self.n_dma_sems + 1), ("d", o.eng, s))
                elif o.sig:
                    ticks[o.eng] += 1
                    o.tok = (esem[o.eng], ticks[o.eng], ("e", o.eng))
            block = st.enter_context(nc.Block())

            def run(e, eng):
                known = {}
                dma_prev = {}
                nd = 0
                for o in per_eng[e]:
                    waits = []
                    for d in o.deps:
                        p = ops[d]
                        if p.tok is None:
                            continue
                        if (not p.dma) and p.eng == "pe" and e == "pe" and not o.dma:
                            continue
                        waits.append(p.tok)
                    if o.dma:
                        s = nd % self.n_dma_sems
                        nd += 1
                        prev = dma_prev.get(s)
                        if prev is not None:
                            waits.append(prev)
                        dma_prev[s] = o.tok
                    best = {}
                    for (sem, val, key) in waits:
                        if known.get(key, 0) >= val:
                            continue
                        if key not in best or best[key][1] < val:
                            best[key] = (sem, val)
                    for key, (sem, val) in best.items():
                        eng.wait_ge(sem, val)
                        known[key] = val
                    ins = o.fn(eng)
                    if o.tok is not None:
                        ins.then_inc(o.tok[0], 16 if o.dma else 1)
                if e == "sp":
                    for i in final_wait_ops:
                        sem, val, key = ops[i].tok
                        if known.get(key, 0) < val:
                            eng.wait_ge(sem, val)
                            known[key] = val

            @block.tensor
            def _(eng):
                run("pe", eng)

            @block.scalar
            def _(eng):
                run("act", eng)

            @block.vector
            def _(eng):
                run("dve", eng)

            @block.gpsimd
            def _(eng):
                run("pool", eng)

            @block.sync
            def _(eng):
                run("sp", eng)

D = 1024
ALPHA = (2 * 2) ** 0.25
LN_EPS = 1e-5
RMS_EPS = 1e-5
F_QA, F_KA, F_VA, F_GA, F_QB, F_KB, F_VB, F_GB, F_QBR, F_KBR = range(10)
NG = 24


def build(S, L=2):
    NBLK = S // 512
    nc = bass.Bass("TRN2", target_bir_lowering=False)
    dt_in = lambda n, s, d=F32: nc.dram_tensor(n, s, d, kind="ExternalInput").ap()
    x_in = dt_in("x", [S, D])
    w_in = dt_in("w_in", [L, D, 4096])
    w_out = dt_in("w_out", [L, D, D])
    rel_bias = dt_in("rel_bias", [L, 8, 257])
    lq1 = dt_in("lambda_q1", [L, 64]); lk1 = dt_in("lambda_k1", [L, 64])
    lq2 = dt_in("lambda_q2", [L, 64]); lk2 = dt_in("lambda_k2", [L, 64])
    subln = dt_in("subln_g", [L, 128])
    ln_g = dt_in("ln_g", [L, D]); ln_b = dt_in("ln_b", [L, D])
    c_ident = dt_in("c_ident", [128, 128], BF16)
    c_antij = dt_in("c_antij", [128, 128])
    c_cs = dt_in("c_cs", [2, 128, S])
    c_valid = dt_in("c_valid", [128, 640], BF16)
    out = nc.dram_tensor("out", [S, D], F32, kind="ExternalOutput").ap()
    wbf = nc.dram_tensor("wbf", [L, NG, 128, 8, 256], BF16).ap()
    x1s = nc.dram_tensor("x1s", [S, D], F32).ap()
    bextd = nc.dram_tensor("bextd", [8, 768], F32).ap()

    T = Tracker()
    st = contextlib.ExitStack()
    with st:
        sb = lambda n, s, d: st.enter_context(nc.sbuf_tensor(n, s, d))
        wbuf = [sb("wbuf%d" % i, [128, 8, 256], BF16) for i in range(4)]
        KB = sb("KB", [128, 4, S], BF16)
        VB = sb("VB", [128, S // 128, 512], BF16)
        KAr = sb("KAr", [128, 4, 1024], BF16)
        VAr = sb("VAr", [128, 8, 512], BF16)
        G = sb("G", [128, 8, 640], BF16)
        xT = sb("xT", [128, 8, 512], BF16)
        QA = sb("QA", [128, 8, 512], BF16)
        QB = sb("QB", [128, 8, 512], BF16)
        gates = sb("gates", [128, 8, 512], BF16)
        yT = sb("yT", [128, 8, 512], BF16)
        cs = sb("cs", [128, 2, 512], F32)
        PT = [sb("PT%d" % i, [128, 512], BF16) for i in range(5)]
        EB = [sb("EB%d" % i, [128, 512], BF16) for i in range(2)]
        xin2 = [sb("xin%d" % i, [128, 1024], F32) for i in range(2)]
        xb2 = [sb("xb%d" % i, [128, 1024], BF16) for i in range(2)]
        zz = sb("zz", [128, 2, 1024], F32)
        lng = sb("lng", [128, 1024], F32)
        lnb = sb("lnb", [128, 1024], F32)
        btmp = sb("btmp", [128, 4, 512], F32)
        recA = sb("recA", [128, 512], F32)
        tA = sb("tA", [128, 512], F32)
        rt1 = sb("rt1", [128, 512], F32)
        rt2 = sb("rt2", [128, 512], F32)
        junk = sb("junk", [128, 1024], BF16)
        ident = sb("ident", [128, 128], BF16)
        antij = sb("antij", [128, 128], F32)
        ones_bf = sb("ones_bf", [128, 128], BF16)
        ones_f = sb("ones_f", [128, 128], F32)
        validA = sb("validA", [128, 640], BF16)
        lamt = sb("lamt", [128, 4, 64], F32)
        sm = sb("sm", [128, 32], F32)
        gbk = [st.enter_context(nc.psum_tensor("gbk%d" % i, [128, 512], F32)) for i in range(4)]
        pT = gbk[3][:, :].bitcast(BF16).rearrange("p (c t) -> p c t", c=8)
        print("sbuf bytes remaining", nc.sbuf_bytes_remaining)
        acc = [st.enter_context(nc.psum_tensor("acc%d" % i, [128, 512], F32)) for i in range(4)]

        C_S1, C_S2, C_E1, C_E2, C_LAM, C_NLAM, C_GSUB, C_SUM, C_SS, C_MEAN, C_VAR, C_RSTD, C_NB, C_MSQ = range(14)
        smc = lambda c: sm[:, c:c + 1]

        gctr = [0]

        def gen_bank():
            i = gctr[0] % 4
            gctr[0] += 1
            return i

        wctr = [0]

        def wslot():
            i = wctr[0] % 4
            wctr[0] += 1
            return i

        pctr = [0]

        def pslot():
            i = pctr[0] % 5
            pctr[0] += 1
            return i

        ectr = [0]

        def eslot():
            i = ectr[0] % 2
            ectr[0] += 1
            return i

        T.op("sp", lambda e: e.dma_start(out=ident[:], in_=c_ident[:, :]), writes=["ident"], dma=True)
        T.op("sp", lambda e: e.dma_start(out=antij[:], in_=c_antij[:, :]), writes=["antij"], dma=True)
        T.op("sp", lambda e: e.dma_start(out=validA[:], in_=c_valid[:, :]), writes=["validA"], dma=True)
        T.op("pool", lambda e: e.memset(ones_bf[:], 1.0), writes=["ones_bf"])
        T.op("pool", lambda e: e.memset(ones_f[:], 1.0 / 128.0), writes=["ones_f"])
        T.op("pool", lambda e: e.memset(QA[:], 0.0), writes=[("QA", i) for i in range(8)])
        T.op("pool", lambda e: e.memset(QB[:], 0.0), writes=[("QB", i) for i in range(8)])

        stg = [zz[:].rearrange("p a (c n) -> p (a c) n", n=256), btmp[:].rearrange("p a (c n) -> p (a c) n", n=256)]
        stg_key = [[("z", 0), ("z", 1)], ["o1", "o2", "r1", "r2"]]
        conv_order = [2, 3, 4, 5, 12, 13, 10, 11, 8, 9, 0, 1, 6, 7, 14, 15, 20, 21, 22, 23]
        k = 0
        OVERLAP = (S // 128 >= 32 and L == 2)
        for l in ([0] if OVERLAP else range(L)):
            for gid in conv_order:
                if gid < 16:
                    f, hf = gid // 2, gid % 2
                    src = w_in[l].rearrange("(c p) n -> p c n", p=128)[:, :, f * 512 + hf * 256: f * 512 + hf * 256 + 256]
                    scale = 0.125 if f in (F_QA, F_QB) else 1.0
                elif gid < 20:
                    continue
                else:
                    gg = gid - 20
                    src = w_out[l].rearrange("(c p) n -> p c n", p=128)[:, :, gg * 256: gg * 256 + 256]
                    scale = 1.0
                    f = -1
                sg = stg[k % 2]; sk = stg_key[k % 2]
                k += 1
                T.op("sp", lambda e, sg=sg, src=src: e.dma_start(out=sg, in_=src), writes=sk, dma=True)
                ws = wslot()
                ceng = "dve" if (k % 2) else "act"
                if ceng == "dve":
                    T.op("dve", lambda e, ws=ws, sg=sg, scale=scale: e.tensor_scalar(
                        out=wbuf[ws][:], in0=sg, scalar1=scale, scalar2=None, op0=ALU.mult),
                        reads=sk, writes=[("w", ws)])
                else:
                    T.op("act", lambda e, ws=ws, sg=sg, scale=scale: e.activation(
                        out=wbuf[ws][:], in_=sg, func=AF.Identity, scale=scale),
                        reads=sk, writes=[("w", ws)])
                T.op("pool", lambda e, ws=ws, l=l, gid=gid: e.dma_start(out=wbf[l, gid], in_=wbuf[ws][:]),
                     reads=[("w", ws)], writes=[("wbf", l, gid)], dma=True)
                if f in (F_QB, F_KB):
                    rgid = (16 if f == F_QB else 18) + hf
                    ws2 = wslot()
                    sv = sg.rearrange("p c (m t d) -> p (c m) t d", t=2, d=32)
                    dv = wbuf[ws2][:].rearrange("p c (m t d) -> p (c m) t d", t=2, d=32)

                    def rotf(e, sv=sv, dv=dv, scale=scale):
                        e.tensor_scalar(out=dv[:, :, 0, :], in0=sv[:, :, 1, :], scalar1=-scale, scalar2=None, op0=ALU.mult)
                        return e.tensor_scalar(out=dv[:, :, 1, :], in0=sv[:, :, 0, :], scalar1=scale, scalar2=None, op0=ALU.mult)

                    def rotf_act(e, sv=sv, dv=dv, scale=scale):
                        e.activation(out=dv[:, :, 0, :], in_=sv[:, :, 1, :], func=AF.Identity, scale=-scale)
                        return e.activation(out=dv[:, :, 1, :], in_=sv[:, :, 0, :], func=AF.Identity, scale=scale)
                    if ceng == "act":
                        T.op("dve", rotf, reads=sk, writes=[("w", ws2)])
                    else:
                        T.op("act", rotf_act, reads=sk, writes=[("w", ws2)])
                    T.op("pool", lambda e, ws2=ws2, l=l, rgid=rgid: e.dma_start(out=wbf[l, rgid], in_=wbuf[ws2][:]),
                         reads=[("w", ws2)], writes=[("wbf", l, rgid)], dma=True)
        out_ops = []
        conv = {"k": 0, "pend": None, "o": 0}
        if OVERLAP:
            cjobs = []
            for gid in list(range(16)) + list(range(20, 24)):
                if gid < 16:
                    f, hf = gid // 2, gid % 2
                    src = w_in[1].rearrange("(c p) n -> p c n", p=128)[:, :, f * 512 + hf * 256: f * 512 + hf * 256 + 256]
                    scale = 0.125 if f in (F_QA, F_QB) else 1.0
                else:
                    gg = gid - 20
                    src = w_out[1].rearrange("(c p) n -> p c n", p=128)[:, :, gg * 256: gg * 256 + 256]
                    scale = 1.0
                cjobs.append((gid, src, scale))
            c_stg = VB[:, 16:24, :].bitcast(F32)
            c_out = [VB[:, 24:28, :].rearrange("p t (a n) -> p (t a) n", n=256), VB[:, 28:32, :].rearrange("p t (a n) -> p (t a) n", n=256)]
            k_stg = [("VB", ti, hf) for ti in range(16, 24) for hf in range(2)]
            k_out = [[("VB", ti, hf) for ti in range(24, 28) for hf in range(2)], [("VB", ti, hf) for ti in range(28, 32) for hf in range(2)]]

        def conv_tick():
            if not OVERLAP:
                return
            if conv["pend"] is not None:
                gid, src, scale = conv["pend"]
                o = conv["o"] % 2
                conv["o"] += 1
                T.op("dve", lambda e, o=o, scale=scale: e.tensor_scalar(out=c_out[o], in0=c_stg, scalar1=scale, scalar2=None, op0=ALU.mult),
                     reads=k_stg, writes=k_out[o])
                T.op("pool", lambda e, o=o, gid=gid: e.dma_start(out=wbf[1, gid], in_=c_out[o]),
                     reads=k_out[o], writes=[("wbf", 1, gid)], dma=True)
                if gid < 16 and (gid // 2) in (F_QB, F_KB):
                    rgid = (16 if gid // 2 == F_QB else 18) + gid % 2
                    o2 = conv["o"] % 2
                    conv["o"] += 1
                    sv = c_stg.rearrange("p c (m t d) -> p (c m) t d", t=2, d=32)
                    dv = c_out[o2].rearrange("p c (m t d) -> p (c m) t d", t=2, d=32)

                    def rotf2(e, sv=sv, dv=dv, scale=scale):
                        e.tensor_scalar(out=dv[:, :, 0, :], in0=sv[:, :, 1, :], scalar1=-scale, scalar2=None, op0=ALU.mult)
                        return e.tensor_scalar(out=dv[:, :, 1, :], in0=sv[:, :, 0, :], scalar1=scale, scalar2=None, op0=ALU.mult)
                    T.op("dve", rotf2, reads=k_stg, writes=k_out[o2])
                    T.op("pool", lambda e, o2=o2, rgid=rgid: e.dma_start(out=wbf[1, rgid], in_=c_out[o2]),
                         reads=k_out[o2], writes=[("wbf", 1, rgid)], dma=True)
                conv["pend"] = None
            if conv["k"] < len(cjobs):
                job = cjobs[conv["k"]]
                conv["k"] += 1
                T.op("pool", lambda e, src=job[1]: e.dma_start(out=c_stg, in_=src), writes=k_stg, dma=True)
                conv["pend"] = job
        def do_layer(l):
            lam_init = 0.8 - 0.6 * math.exp(-0.3 * l)
            src_x = x_in if l == 0 else x1s
            dst_x = out if l == L - 1 else x1s
            src_key = "xin_d" if l == 0 else "x1s"
            dst_key = "out_d" if l == L - 1 else "x1s"

            for i, v in enumerate((lq1, lk1, lq2, lk2)):
                bsrc = bass.AP(v.tensor, l * 64, [[0, 128], [1, 64]])
                T.op("sp", lambda e, i=i, bsrc=bsrc: e.dma_start(out=lamt[:, i, :], in_=bsrc), writes=[("lamt", i)], dma=True)
            T.op("dve", lambda e: e.tensor_tensor(out=lamt[:, 0, :], in0=lamt[:, 0, :], in1=lamt[:, 1, :], op=ALU.mult),
                 reads=[("lamt", 0), ("lamt", 1)], writes=[("lamt", 0)])
            T.op("dve", lambda e: e.tensor_tensor(out=lamt[:, 2, :], in0=lamt[:, 2, :], in1=lamt[:, 3, :], op=ALU.mult),
                 reads=[("lamt", 2), ("lamt", 3)], writes=[("lamt", 2)])
            T.op("dve", lambda e: e.reduce_sum(out=smc(C_S1), in_=lamt[:, 0, :], axis=AX.X), reads=[("lamt", 0)], writes=["s1"])
            T.op("dve", lambda e: e.reduce_sum(out=smc(C_S2), in_=lamt[:, 2, :], axis=AX.X), reads=[("lamt", 2)], writes=["s2"])
            T.op("act", lambda e: e.activation(out=smc(C_E1), in_=smc(C_S1), func=AF.Exp), reads=["s1"], writes=["e1"])
            T.op("act", lambda e: e.activation(out=smc(C_E2), in_=smc(C_S2), func=AF.Exp), reads=["s2"], writes=["e2"])
            T.op("dve", lambda e: e.tensor_tensor(out=smc(C_LAM), in0=smc(C_E1), in1=smc(C_E2), op=ALU.subtract),
                 reads=["e1", "e2"], writes=["lam"])
            T.op("dve", lambda e, li=lam_init: e.tensor_scalar(out=smc(C_NLAM), in0=smc(C_LAM), scalar1=-1.0, scalar2=-li,
                                                             op0=ALU.mult, op1=ALU.add), reads=["lam"], writes=["nlam"])
            gsrc = bass.AP(subln.tensor, l * 128, [[1, 128], [1, 1]])
            T.op("sp", lambda e, gsrc=gsrc: e.dma_start(out=smc(C_GSUB), in_=gsrc), writes=["gsub0"], dma=True)
            T.op("dve", lambda e, li=lam_init: e.tensor_scalar(out=smc(C_GSUB), in0=smc(C_GSUB), scalar1=1.0 - li, scalar2=None,
                                                             op0=ALU.mult), reads=["gsub0"], writes=["gsub"])
            T.op("sp", lambda e, l=l: e.dma_start(out=lng[:], in_=bass.AP(ln_g.tensor, l * D, [[0, 128], [1, D]])), writes=["lng"], dma=True)
            T.op("sp", lambda e, l=l: e.dma_start(out=lnb[:], in_=bass.AP(ln_b.tensor, l * D, [[0, 128], [1, D]])), writes=["lnb"], dma=True)
            def build_G():
                rb = btmp[0:8, 0, 0:257]
                bx = btmp[0:8, 1, :]
                bx2 = btmp[0:8, 2, 0:256]
                zr = btmp[0:8, 3, :]
                T.op("sp", lambda e, l=l: e.dma_start(out=rb, in_=rel_bias[l]), writes=["o1"], dma=True)
                T.op("pool", lambda e: e.memset(zr, 0.0), writes=["r2"])
                T.op("dve", lambda e: e.tensor_copy(out=bx[:, 0:256], in_=btmp[0:8, 0, 1:257]), reads=["o1"], writes=["o2"])
                T.op("dve", lambda e: e.tensor_scalar(out=bx[:, 256:512], in0=zr[:, 0:256], scalar1=btmp[0:8, 0, 256:257], scalar2=None,
                                                      op0=ALU.add), reads=["o1", "r2"], writes=["o2"])
                T.op("dve", lambda e: e.tensor_scalar(out=bx2, in0=zr[:, 0:256], scalar1=btmp[0:8, 0, 256:257], scalar2=None,
                                                      op0=ALU.add), reads=["o1", "r2"], writes=["r1"])
                T.op("pool", lambda e: e.dma_start(out=bextd[:, 0:512], in_=bx), reads=["o2", "o2"], writes=["bextd0"], dma=True)
                T.op("pool", lambda e: e.dma_start(out=bextd[:, 512:768], in_=bx2), reads=["r1"], writes=["bextd1"], dma=True)
                for h in range(8):
                    hk = zz[:, 0, 0:640]
                    ex = zz[:, 1, 0:640]
                    hsrc = bass.AP(bextd.tensor, h * 768, [[1, 128], [1, 640]])
                    T.op("sp", lambda e, hsrc=hsrc, hk=hk: e.dma_start(out=hk, in_=hsrc), reads=["bextd0", "bextd1"], writes=[("z", 0)], dma=True)
                    b0, b1 = gen_bank(), gen_bank()

                    def mj(e, b0=b0, b1=b1, hk=hk):
                        e.matmul(gbk[b0][:, 0:512], lhsT=antij[:], rhs=hk[:, 0:512], start=True, stop=True)
                        return e.matmul(gbk[b1][:, 0:128], lhsT=antij[:], rhs=hk[:, 512:640], start=True, stop=True)
                    T.op("pe", mj, reads=[("z", 0), "antij"], writes=[("g", b0), ("g", b1)])
                    T.op("act", lambda e, b0=b0, ex=ex: e.activation(out=ex[:, 0:512], in_=gbk[b0][:, 0:512], func=AF.Exp),
                         reads=[("g", b0)], writes=[("z", 1)])
                    T.op("act", lambda e, b1=b1, ex=ex: e.activation(out=ex[:, 512:640], in_=gbk[b1][:, 0:128], func=AF.Exp),
                         reads=[("g", b1)], writes=[("z", 1)])
                    T.op("dve", lambda e, h=h, ex=ex: e.tensor_tensor(out=G[:, h, :], in0=ex, in1=validA[:], op=ALU.mult),
                         reads=[("z", 1), ("z", 1), "validA"], writes=[("G", h)])

            def stage_T(j):
                def ld(i):
                    r0 = j * 512 + i * 128
                    xin = xin2[i % 2]
                    T.op("sp", lambda e: e.dma_start(out=xin[:], in_=src_x[r0:r0 + 128, :]),
                         reads=[(src_key, r0 // 128)], writes=[("xin", i % 2)], dma=True)
                ld(0)
                ld(1)
                for i in range(4):
                    xin = xin2[i % 2]; xb = xb2[i % 2]
                    T.op("dve", lambda e, xin=xin, xb=xb: e.tensor_copy(out=xb[:], in_=xin[:]), reads=[("xin", i % 2)], writes=[("xb", i % 2)])
                    if i + 2 < 4:
                        ld(i + 2)

                    def tr(e, xb=xb):
                        for c in range(8):
                            ins = e.transpose(out=pT[:, c, :], in_=xb[:, c * 128:(c + 1) * 128], identity=ident[:])
                        return ins
                    T.op("pe", tr, reads=[("xb", i % 2), "ident"], writes=[("g", 3)])
                    T.op("act" if i % 2 else "dve",
                         (lambda e, i=i: e.copy(out=xT[:, :, i * 128:(i + 1) * 128], in_=pT[:])) if i % 2 else
                         (lambda e, i=i: e.tensor_copy(out=xT[:, :, i * 128:(i + 1) * 128], in_=pT[:])),
                         reads=[("g", 3)], writes=[("xT", i)])
                T.op("sp", lambda e, j=j: e.dma_start(out=cs[:, 0, :], in_=c_cs[0, :, j * 512:(j + 1) * 512]), writes=[("cs", 0)], dma=True)
                T.op("sp", lambda e, j=j: e.dma_start(out=cs[:, 1, :], in_=c_cs[1, :, j * 512:(j + 1) * 512]), writes=[("cs", 1)], dma=True)

            xT_keys = [("xT", i) for i in range(4)]

            def load_w(gid):
                ws = wslot()
                T.op("sp", lambda e, ws=ws, gid=gid: e.dma_start(out=wbuf[ws][:], in_=wbf[l, gid]),
                     reads=[("wbf", l, gid)], writes=[("w", ws)], dma=True)
                return ws

            def fm_tile(ws, m, b):
                def f(e):
                    for c in range(8):
                        ins = e.matmul(gbk[b][:, :], lhsT=wbuf[ws][:, c, m * 128:(m + 1) * 128], rhs=xT[:, c, :],
                                       start=(c == 0), stop=(c == 7))
                    return ins
                T.op("pe", f, reads=[("w", ws)] + xT_keys, writes=[("g", b)])

            def tm_tile(ws, i, b):
                def f(e):
                    for c in range(8):
                        ins = e.matmul(gbk[b][:, 0:256], lhsT=xT[:, c, i * 128:(i + 1) * 128], rhs=wbuf[ws][:, c, :],
                                       start=(c == 0), stop=(c == 7))
                    return ins
                T.op("pe", f, reads=[("w", ws), ("xT", i)], writes=[("g", b)])

            def stage_P(j):
                slot = j % 2
                t0 = j * 512
                ev = [0]

                def cp_eng():
                    ev[0] += 1
                    return "act" if ev[0] % 2 else "dve"

                def copy_op(eng, o, i_, reads, writes):
                    if eng == "act":
                        T.op("act", lambda e: e.copy(out=o, in_=i_), reads=reads, writes=writes)
                    else:
                        T.op("dve", lambda e: e.tensor_copy(out=o, in_=i_), reads=reads, writes=writes)
                for hf in range(2):
                    ws = load_w(2 * F_KA + hf)
                    for m in range(2):
                        hp = 2 * hf + m
                        b = gen_bank()
                        fm_tile(ws, m, b)
                        copy_op(cp_eng(), KAr[:, hp, slot * 512:(slot + 1) * 512], gbk[b][:, :], [("g", b)], [("KAr", slot, hp)])
                for f_, dstT, dkey in ((F_VA, VAr, "VAr"), (F_VB, VB, "VB")):
                    for hf in range(2):
                        ws = load_w(2 * f_ + hf)
                        for i in range(4):
                            b = gen_bank()
                            tm_tile(ws, i, b)
                            ti = (slot * 4 + i) if f_ == F_VA else (j * 4 + i)
                            copy_op(cp_eng(), dstT[:, ti, hf * 256:(hf + 1) * 256], gbk[b][:, 0:256], [("g", b)], [(dkey, ti, hf)])
                for f_, fr_ in ((F_KB, F_KBR), (F_QB, F_QBR)):
                    for hf in range(2):
                        ws = load_w(2 * f_ + hf)
                        ws2 = load_w(2 * fr_ + hf)
                        for m in range(2):
                            h = 2 * hf + m
                            b = gen_bank(); b2 = gen_bank()
                            fm_tile(ws, m, b)
                            fm_tile(ws2, m, b2)
                            T.op("dve", lambda e, b=b: e.tensor_tensor(out=rt1[:], in0=gbk[b][:, :], in1=cs[:, 0, :], op=ALU.mult),
                                 reads=[("g", b), ("cs", 0)], writes=["rt1"])
                            T.op("dve", lambda e, b2=b2: e.tensor_tensor(out=rt2[:], in0=gbk[b2][:, :], in1=cs[:, 1, :], op=ALU.mult),
                                 reads=[("g", b2), ("cs", 1)], writes=["rt2"])
                            if f_ == F_KB:
                                T.op("dve", lambda e, h=h: e.tensor_tensor(out=KB[:, h, t0:t0 + 512], in0=rt1[:], in1=rt2[:], op=ALU.add),
                                     reads=["rt1", "rt2"], writes=[("KB", h, j)])
                            else:
                                def qadd(e, h=h):
                                    e.tensor_tensor(out=QB[0:64, 2 * h, :], in0=rt1[0:64, :], in1=rt2[0:64, :], op=ALU.add)
                                    return e.tensor_tensor(out=QB[64:128, 2 * h + 1, :], in0=rt1[64:128, :], in1=rt2[64:128, :], op=ALU.add)
                                T.op("dve", qadd, reads=["rt1", "rt2"], writes=[("QB", 2 * h), ("QB", 2 * h + 1)])
                for hf in range(2):
                    ws = load_w(2 * F_QA + hf)
                    for m in range(2):
                        hp = 2 * hf + m
                        b = gen_bank()
                        fm_tile(ws, m, b)
                        copy_op("act", QA[0:64, 2 * hp, :], gbk[b][0:64, :], [("g", b)], [("QA", 2 * hp)])
                        copy_op("dve", QA[64:128, 2 * hp + 1, :], gbk[b][64:128, :], [("g", b)], [("QA", 2 * hp + 1)])
                for f_, off in ((F_GA, 0), (F_GB, 4)):
                    for hf in range(2):
                        ws = load_w(2 * f_ + hf)
                        for m in range(2):
                            ch = off + 2 * hf + m
                            b = gen_bank()
                            fm_tile(ws, m, b)
                            T.op("act", lambda e, b=b, ch=ch: e.activation(out=gates[:, ch, :], in_=gbk[b][:, :], func=AF.Silu),
                                 reads=[("g", b)], writes=[("gates", ch)])

            def pipeline(units, la=4):
                n = len(units)
                for i in range(n + la):
                    if i < n:
                        units[i][0]()
                    if i - la >= 0:
                        u = units[i - la]
                        u[1]()
                        if len(u) > 2:
                            u[2]()

            def stage_A(j):
                us = []
                for h in range(8):
                    us += head_A(j, h)
                return us

            def head_A(j, h):
                if True:
                    hp, par = h // 2, h % 2
                    ob, db = (0, 1) if par == 0 else (2, 3)
                    rows = slice(0, 64) if par == 0 else slice(64, 128)
                    rs = [r for r in range(8) if not (j == 0 and r < 4)]
                    units = []
                    for ui, r in enumerate(rs):
                        if r < 4:
                            kslot, ktile = (j - 1) % 2, r
                            q0, N, u0 = 0, 128 * (r + 1), 512 - 128 * r
                        else:
                            kslot, ktile = j % 2, r - 4
                            q0 = 128 * (r - 4); N = 512 - q0; u0 = 0
                        st_ = {}

                        def sc(r=r, kslot=kslot, ktile=ktile, q0=q0, N=N, u0=u0, st_=st_):
                            b = gen_bank(); es = eslot(); ps = pslot()
                            st_["ps"] = ps
                            T.op("pe", lambda e: e.matmul(gbk[b][:, 0:N], lhsT=KAr[:, hp, kslot * 512 + ktile * 128: kslot * 512 + ktile * 128 + 128],
                                                          rhs=QA[:, h, q0:q0 + N], start=True, stop=True),
                                 reads=[("KAr", kslot, hp), ("QA", h)], writes=[("g", b)])
                            T.op("act", lambda e: e.activation(out=EB[es][:, 0:N], in_=gbk[b][:, 0:N], func=AF.Exp),
                                 reads=[("g", b)], writes=[("EB", es)])
                            T.op("dve", lambda e: e.tensor_tensor(out=PT[ps][:, 0:N], in0=EB[es][:, 0:N], in1=G[:, h, u0:u0 + N], op=ALU.mult),
                                 reads=[("EB", es), ("G", h)], writes=[("PT", ps)])

                        def pv(ui=ui, kslot=kslot, ktile=ktile, q0=q0, N=N, st_=st_, last=(ui == len(rs) - 1)):
                            ps = st_["ps"]

                            def f(e):
                                e.matmul(acc[ob][:, q0:q0 + N], lhsT=VAr[:, kslot * 4 + ktile, hp * 128:(hp + 1) * 128], rhs=PT[ps][:, 0:N],
                                         start=(ui == 0), stop=last, skip_group_check=True)
                                return e.matmul(acc[db][:, q0:q0 + N], lhsT=ones_bf[:], rhs=PT[ps][:, 0:N],
                                                start=(ui == 0), stop=last, skip_group_check=True)
                            T.op("pe", f, reads=[("PT", ps), ("VAr", kslot * 4 + ktile, hp // 2), "ones_bf"],
                                 writes=[("acc", ob), ("acc", db)])
                        units.append((sc, pv))
                    def epi():
                        T.op("act", lambda e, db=db, rows=rows: e.activation(out=recA[rows, :], in_=acc[db][rows, :], func=AF.Ln),
                             reads=[("acc", db)], writes=[("recA", par)])
                        T.op("act", lambda e, rows=rows: e.activation(out=recA[rows, :], in_=recA[rows, :], func=AF.Exp, scale=-1.0),
                             reads=[("recA", par)], writes=[("recA", par)])
                        T.op("dve", lambda e, ob=ob, rows=rows: e.tensor_tensor(out=tA[rows, :], in0=acc[ob][rows, :], in1=recA[rows, :], op=ALU.mult),
                             reads=[("acc", ob), ("recA", par)], writes=[("tA", par)])
                        T.op("dve", lambda e, rows=rows, hp=hp: e.tensor_tensor(out=yT[rows, hp, :], in0=tA[rows, :], in1=gates[rows, hp, :], op=ALU.mult),
                             reads=[("tA", par), ("gates", hp)], writes=[("yT", hp, par)])
                    units[-1] = (units[-1][0], units[-1][1], epi)
                    return units

            def stage_B(j):
                us = []
                for h in range(4):
                    us += head_B(j, h)
                return us

            def head_B(j, h):
                if True:
                    kts = [(4 * j + a, 128 * a, True) for a in range(4)] + [(kt, 0, False) for kt in range(4 * j)]
                    units = []
                    nu = 2 * len(kts)
                    for ki, (kt, q0, diag) in enumerate(kts):
                        for n in range(2):
                            ui = 2 * ki + n
                            N = 512 - q0
                            st_ = {}

                            def sc(kt=kt, q0=q0, N=N, n=n, diag=diag, st_=st_):
                                b = gen_bank(); ps = pslot()
                                st_["ps"] = ps
                                T.op("pe", lambda e: e.matmul(gbk[b][:, 0:N], lhsT=KB[:, h, kt * 128:(kt + 1) * 128],
                                                              rhs=QB[:, 2 * h + n, q0:512], start=True, stop=True),
                                     reads=[("KB", h, kt // 4), ("QB", 2 * h + n)], writes=[("g", b)])
                                T.op("act", lambda e: e.activation(out=PT[ps][:, 0:N], in_=gbk[b][:, 0:N], func=AF.Exp),
                                     reads=[("g", b)], writes=[("PT", ps)])
                                if diag:
                                    T.op("pool", lambda e: e.memset(PT[ps][64:128, 0:64], 0.0), reads=[("PT", ps)], writes=[("PT", ps)])

                            def pv(kt=kt, q0=q0, N=N, n=n, st_=st_, first=(ki == 0), last=(ki == len(kts) - 1)):
                                ps = st_["ps"]

                                def f(e):
                                    e.matmul(acc[n][:, q0:512], lhsT=VB[:, kt, h * 128:(h + 1) * 128], rhs=PT[ps][:, 0:N],
                                             start=first, stop=last, skip_group_check=True)
                                    return e.matmul(acc[2 + n][:, q0:512], lhsT=ones_bf[:], rhs=PT[ps][:, 0:N],
                                                    start=first, stop=last, skip_group_check=True)
                                T.op("pe", f, reads=[("PT", ps), ("VB", kt, h // 2), "ones_bf"], writes=[("acc", n), ("acc", 2 + n)])
                            units.append((sc, pv))
                    def epi():
                        o1, o2, r1, r2 = btmp[:, 0, :], btmp[:, 1, :], btmp[:, 2, :], btmp[:, 3, :]
                        T.op("act", lambda e: e.activation(out=r1, in_=acc[2][:, :], func=AF.Ln), reads=[("acc", 2)], writes=["r1"])
                        T.op("act", lambda e: e.activation(out=r2, in_=acc[3][:, :], func=AF.Ln), reads=[("acc", 3)], writes=["r2"])
                        T.op("act", lambda e: e.activation(out=r1, in_=r1, func=AF.Exp, scale=-1.0), reads=["r1"], writes=["r1"])
                        T.op("act", lambda e: e.activation(out=r2, in_=r2, func=AF.Exp, scale=-1.0), reads=["r2"], writes=["r2"])
                        T.op("dve", lambda e: e.tensor_tensor(out=o1, in0=acc[0][:, :], in1=r1, op=ALU.mult), reads=[("acc", 0), "r1"], writes=["o1"])
                        T.op("dve", lambda e: e.tensor_tensor(out=o2, in0=acc[1][:, :], in1=r2, op=ALU.mult), reads=[("acc", 1), "r2"], writes=["o2"])
                        T.op("dve", lambda e: e.scalar_tensor_tensor(out=o1, in0=o2, scalar=smc(C_NLAM), in1=o1, op0=ALU.mult, op1=ALU.add),
                             reads=["o1", "o2", "nlam"], writes=["o1"])
                        T.op("dve", lambda e: e.tensor_tensor(out=r1, in0=o1, in1=o1, op=ALU.mult), reads=["o1"], writes=["r1"])
                        b = gen_bank()
                        T.op("pe", lambda e, b=b: e.matmul(gbk[b][:, :], lhsT=ones_f[:], rhs=r1, start=True, stop=True),
                             reads=["r1", "ones_f"], writes=[("g", b)])
                        T.op("act", lambda e, b=b: e.activation(out=r2, in_=gbk[b][:, :], func=AF.Ln, bias=RMS_EPS), reads=[("g", b)], writes=["r2"])
                        T.op("act", lambda e: e.activation(out=r2, in_=r2, func=AF.Exp, scale=-0.5), reads=["r2"], writes=["r2"])
                        T.op("dve", lambda e: e.tensor_tensor(out=o1, in0=o1, in1=r2, op=ALU.mult), reads=["o1", "r2"], writes=["o1"])
                        T.op("dve", lambda e, h=h: e.scalar_tensor_tensor(out=yT[:, 4 + h, :], in0=o1, scalar=smc(C_GSUB), in1=gates[:, 4 + h, :],
                                                                     op0=ALU.mult, op1=ALU.mult),
                             reads=["o1", "gsub", ("gates", 4 + h)], writes=[("yT", 4 + h, 0), ("yT", 4 + h, 1)])
                    units[-1] = (units[-1][0], units[-1][1], epi)
                    return units

            yT_keys = [("yT", c, p) for c in range(8) for p in range(2)]

            def stage_O(j):
                wss = [load_w(20 + gg) for gg in range(4)]
                for i in range(4):
                    r0 = j * 512 + i * 128
                    zb = i % 2
                    z = zz[:, zb, :]
                    T.op("sp", lambda e, r0=r0, z=z: e.dma_start(out=z, in_=src_x[r0:r0 + 128, :]),
                         reads=[(src_key, r0 // 128)], writes=[("z", zb)], dma=True)
                    for gg in range(4):
                        b = gen_bank()

                        def f(e, b=b, gg=gg, i=i):
                            for c in range(8):
                                ins = e.matmul(gbk[b][:, 0:256], lhsT=yT[:, c, i * 128:(i + 1) * 128], rhs=wbuf[wss[gg]][:, c, :],
                                               start=(c == 0), stop=(c == 7))
                            return ins
                        T.op("pe", f, reads=[("w", wss[gg])] + yT_keys, writes=[("g", b)])
                        T.op("dve", lambda e, b=b, gg=gg, z=z: e.scalar_tensor_tensor(
                            out=z[:, gg * 256:(gg + 1) * 256], in0=z[:, gg * 256:(gg + 1) * 256], scalar=ALPHA, in1=gbk[b][:, 0:256],
                            op0=ALU.mult, op1=ALU.add), reads=[("g", b), ("z", zb)], writes=[("z", zb)])
                    T.op("dve", lambda e, z=z: e.reduce_sum(out=smc(C_SUM), in_=z, axis=AX.X), reads=[("z", zb)], writes=["sum"])
                    T.op("act", lambda e, z=z: e.activation(out=junk[:], in_=z, func=AF.Square), reads=[("z", zb)], writes=["junk"])
                    T.op("dve", lambda e: e.reduce_sum(out=smc(C_SS), in_=junk[:], axis=AX.X), reads=["junk"], writes=["ss"])
                    T.op("dve", lambda e: e.tensor_scalar(out=smc(C_MEAN), in0=smc(C_SUM), scalar1=1.0 / D, scalar2=None, op0=ALU.mult),
                         reads=["sum"], writes=["mean"])
                    T.op("dve", lambda e: e.tensor_tensor(out=smc(C_MSQ), in0=smc(C_MEAN), in1=smc(C_MEAN), op=ALU.mult),
                         reads=["mean"], writes=["msq"])
                    T.op("dve", lambda e: e.scalar_tensor_tensor(out=smc(C_VAR), in0=smc(C_SS), scalar=1.0 / D, in1=smc(C_MSQ),
                                                                 op0=ALU.mult, op1=ALU.subtract), reads=["ss", "msq"], writes=["var"])
                    T.op("act", lambda e: e.activation(out=smc(C_RSTD), in_=smc(C_VAR), func=AF.Ln, bias=LN_EPS), reads=["var"], writes=["rstd"])
                    T.op("act", lambda e: e.activation(out=smc(C_RSTD), in_=smc(C_RSTD), func=AF.Exp, scale=-0.5), reads=["rstd"], writes=["rstd"])
                    T.op("dve", lambda e: e.scalar_tensor_tensor(out=smc(C_NB), in0=smc(C_MEAN), scalar=-1.0, in1=smc(C_RSTD),
                                                                 op0=ALU.mult, op1=ALU.mult), reads=["mean", "rstd"], writes=["nb"])
                    T.op("act", lambda e, z=z: e.activation(out=z, in_=z, func=AF.Identity, bias=smc(C_NB), scale=smc(C_RSTD)),
                         reads=[("z", zb), "nb", "rstd"], writes=[("z", zb)])
                    T.op("dve", lambda e, z=z: e.tensor_tensor(out=z, in0=z, in1=lng[:], op=ALU.mult), reads=[("z", zb), "lng"], writes=[("z", zb)])
                    T.op("dve", lambda e, z=z: e.tensor_tensor(out=z, in0=z, in1=lnb[:], op=ALU.add), reads=[("z", zb), "lnb"], writes=[("z", zb)])
                    o = T.op("pool", lambda e, r0=r0, z=z: e.dma_start(out=dst_x[r0:r0 + 128, :], in_=z),
                             reads=[("z", zb)], writes=[(dst_key, r0 // 128)], dma=True)
                    if l == L - 1:
                        out_ops.append(o.idx)

            stage_T(0)
            stage_P(0)
            build_G()
            for j in range(NBLK):
                if j + 1 < NBLK:
                    stage_T(j + 1)
                if l == 0 and j < 4:
                    us = stage_A(j)
                    us.append((conv_tick, lambda: None))
                    for h in range(4):
                        us += head_B(j, h)
                        us.append((conv_tick, lambda: None))
                    pipeline(us)
                    if j == 3:
                        while conv["pend"] is not None or (OVERLAP and conv["k"] < len(cjobs)):
                            conv_tick()
                else:
                    pipeline(stage_A(j) + stage_B(j))
                stage_O(j)
                if j + 1 < NBLK:
                    stage_P(j + 1)
        for l in range(L):
            do_layer(l)
        T.emit(nc, final_wait_ops=out_ops)
    return nc


def make_consts(S):
    ident = np.eye(128, dtype=np.float32).astype(ml_dtypes.bfloat16)
    antij = np.ascontiguousarray(np.eye(128, dtype=np.float32)[::-1])
    inv_freq = 10000.0 ** (-np.arange(0, 64, 2, dtype=np.float32) / 64.0)
    pos = np.arange(S, dtype=np.float32)
    ang = pos[None, :] * np.concatenate([inv_freq, inv_freq])[:, None]
    ang = np.concatenate([ang, ang], axis=0).astype(np.float32)
    cs = np.stack([np.cos(ang), np.sin(ang)]).astype(np.float32)
    kk = np.arange(128)[:, None] // 64
    uu = np.arange(640)[None, :] // 64
    valid = ((uu - kk >= 0) & (uu - kk <= 8)).astype(np.float32).astype(ml_dtypes.bfloat16)
    return {"c_ident": ident, "c_antij": antij, "c_cs": cs, "c_valid": valid}


_CACHE = {}


def kernel(x, w_in, w_out, rel_bias, lambda_q1, lambda_k1, lambda_q2, lambda_k2, subln_g, ln_g, ln_b):
    x = np.asarray(x, dtype=np.float32)
    B, S, _ = x.shape
    L = int(np.asarray(w_in).shape[0])
    key = (S, L)
    if key not in _CACHE:
        _CACHE[key] = build(S, L)
    nc = _CACHE[key]
    consts = make_consts(S)
    shared = {"w_in": np.ascontiguousarray(w_in, dtype=np.float32), "w_out": np.ascontiguousarray(w_out, dtype=np.float32),
              "rel_bias": np.ascontiguousarray(rel_bias, dtype=np.float32),
              "lambda_q1": np.ascontiguousarray(lambda_q1, dtype=np.float32), "lambda_k1": np.ascontiguousarray(lambda_k1, dtype=np.float32),
              "lambda_q2": np.ascontiguousarray(lambda_q2, dtype=np.float32), "lambda_k2": np.ascontiguousarray(lambda_k2, dtype=np.float32),
              "subln_g": np.ascontiguousarray(subln_g, dtype=np.float32),
              "ln_g": np.ascontiguousarray(ln_g, dtype=np.float32), "ln_b": np.ascontiguousarray(ln_b, dtype=np.float32)}
    shared.update(consts)
    in_maps = [dict(shared, x=np.ascontiguousarray(x[b])) for b in range(B)]
    res = run_bass_kernel_spmd(nc, in_maps, core_ids=list(range(B)))
    return np.stack([np.asarray(r["out"], dtype=np.float32) for r in res.results], axis=0)
```
